# Optimizing a Trainium2 kernel written in Bass

```python
import math
import jax, jax.numpy as jnp
from jax import lax
import numpy as np

D_MODEL = 1024
BATCH = 8
SEQ = 4096
DEPTH = 4

CHUNK = 128

MLSTM_HEADS = 4
MLSTM_INNER = D_MODEL // 2
MLSTM_V = MLSTM_INNER // MLSTM_HEADS
MLSTM_QK = MLSTM_V // 2
RET_HEADS = 4
RET_INNER = D_MODEL // 2
RET_V = RET_INNER // RET_HEADS
RET_QK = RET_V // 2
ROPE_BASE = 10000.0
HYB_INNER = MLSTM_INNER + RET_INNER
HYB_SIZES = (MLSTM_HEADS * MLSTM_QK, MLSTM_HEADS * MLSTM_QK, MLSTM_INNER, MLSTM_HEADS, MLSTM_HEADS, MLSTM_INNER,
             RET_HEADS * RET_QK, RET_HEADS * RET_QK, RET_INNER, RET_INNER)
HYB_PROJ = sum(HYB_SIZES)
HYB_SPLITS = tuple(sum(HYB_SIZES[:i + 1]) for i in range(len(HYB_SIZES) - 1))

SSD_INNER = 2 * D_MODEL
SSD_HEADDIM = 64
SSD_HEADS = SSD_INNER // SSD_HEADDIM
SSD_GROUPS = 8
SSD_STATE = 128
SSD_CONV = 4
SSD_CONV_DIM = SSD_INNER + 2 * SSD_GROUPS * SSD_STATE
SSD_PROJ = SSD_INNER + SSD_CONV_DIM + SSD_HEADS

D_FF = 4 * D_MODEL

kernel_name = "hybrid_mlstm_retention_ssd_trunk"

F32 = jnp.float32


def rmsnorm(x, w, eps=1e-6):
    xf = x.astype(F32)
    y = xf * lax.rsqrt(jnp.mean(xf * xf, axis=-1, keepdims=True) + eps)
    return (y * w.astype(F32)).astype(x.dtype)


def headnorm(h, w, eps=1e-5):
    mu = jnp.mean(h, axis=-1, keepdims=True)
    var = jnp.mean(jnp.square(h - mu), axis=-1, keepdims=True)
    y = (h - mu) * lax.rsqrt(var + eps)
    bsz, seq, nh, d = h.shape
    return y.reshape(bsz, seq, nh * d) * w.astype(F32)


def rotary(t):
    seq, d = t.shape[1], t.shape[-1]
    inv = ROPE_BASE ** (-jnp.arange(0, d, 2, dtype=F32) / d)
    ang = jnp.arange(seq, dtype=F32)[:, None] * inv[None, :]
    cos = jnp.cos(ang)[None, :, None, :]
    sin = jnp.sin(ang)[None, :, None, :]
    t1, t2 = t[..., : d // 2], t[..., d // 2:]
    return jnp.concatenate([t1 * cos - t2 * sin, t1 * sin + t2 * cos], axis=-1)


def causal_mask():
    return jnp.tril(jnp.ones((CHUNK, CHUNK), dtype=bool))


def mlstm_chunkwise(q, k, v, i_pre, f_pre):
    bsz, seq, nh, dk = q.shape
    dv = v.shape[-1]
    nc = seq // CHUNK
    q = (q * dk ** -0.5).reshape(bsz, nc, CHUNK, nh, dk)
    k = k.reshape(bsz, nc, CHUNK, nh, dk)
    v = v.reshape(bsz, nc, CHUNK, nh, dv)
    ig = i_pre.reshape(bsz, nc, CHUNK, nh)
    b = jnp.cumsum(jax.nn.log_sigmoid(f_pre).reshape(bsz, nc, CHUNK, nh), axis=2)
    g = b[:, :, -1]
    mask = causal_mask()[None, None, :, :, None]
    log_d = jnp.where(mask, b[:, :, :, None, :] - b[:, :, None, :, :] + ig[:, :, None, :, :], -jnp.inf)
    log_w = g[:, :, None, :] - b + ig
    a = jnp.max(log_w, axis=2)
    w = jnp.exp(log_w - a[:, :, None, :])
    d_c = jnp.einsum('bclh,bclhk,bclhv->bchkv', w, k, v)
    d_n = jnp.einsum('bclh,bclhk->bchk', w, k)

    def step(carry, inp):
        c, n, m = carry
        g_c, a_c, dc, dn = inp
        m_new = jnp.maximum(g_c + m, a_c)
        s_old = jnp.exp(g_c + m - m_new)
        s_new = jnp.exp(a_c - m_new)
        c_new = s_old[..., None, None] * c + s_new[..., None, None] * dc
        n_new = s_old[..., None] * n + s_new[..., None] * dn
        return (c_new, n_new, m_new), (c, n, m)

    init = (jnp.zeros((bsz, nh, dk, dv), F32), jnp.zeros((bsz, nh, dk), F32), jnp.zeros((bsz, nh), F32))
    xs = (jnp.moveaxis(g, 1, 0), jnp.moveaxis(a, 1, 0), jnp.moveaxis(d_c, 1, 0), jnp.moveaxis(d_n, 1, 0))
    _, (c_prev, n_prev, m_prev) = lax.scan(step, init, xs)
    c_prev = jnp.moveaxis(c_prev, 0, 1)
    n_prev = jnp.moveaxis(n_prev, 0, 1)
    m_prev = jnp.moveaxis(m_prev, 0, 1)

    m_inter = b + m_prev[:, :, None, :]
    m_t = jnp.maximum(m_inter, jnp.max(log_d, axis=3))
    s = jnp.einsum('bcthk,bcshk->bctsh', q, k) * jnp.exp(log_d - m_t[:, :, :, None, :])
    scale_inter = jnp.exp(m_inter - m_t)
    num = jnp.einsum('bctsh,bcshv->bcthv', s, v) + scale_inter[..., None] * jnp.einsum('bcthk,bchkv->bcthv', q, c_prev)
    den = jnp.sum(s, axis=3) + scale_inter * jnp.einsum('bcthk,bchk->bcth', q, n_prev)
    h = num / jnp.maximum(jnp.abs(den), jnp.exp(-m_t))[..., None]
    return h.reshape(bsz, seq, nh, dv)


def retention_chunkwise(q, k, v):
    bsz, seq, nh, dk = q.shape
    dv = v.shape[-1]
    nc = seq // CHUNK
    log_gamma = jnp.log(1.0 - 2.0 ** (-5.0 - jnp.arange(nh, dtype=F32)))
    q = (q * dk ** -0.5).reshape(bsz, nc, CHUNK, nh, dk)
    k = k.reshape(bsz, nc, CHUNK, nh, dk)
    v = v.reshape(bsz, nc, CHUNK, nh, dv)
    idx = jnp.arange(CHUNK, dtype=F32)
    rel = idx[:, None] - idx[None, :]
    decay = jnp.where(causal_mask()[:, :, None], jnp.exp(jnp.maximum(rel, 0.0)[:, :, None] * log_gamma), 0.0)
    s = jnp.einsum('bcthk,bcshk->bctsh', q, k) * decay
    intra = jnp.einsum('bctsh,bcshv->bcthv', s, v)
    w_k = jnp.exp((CHUNK - 1.0 - idx)[:, None] * log_gamma)
    d_r = jnp.einsum('lh,bclhk,bclhv->bchkv', w_k, k, v)
    chunk_decay = jnp.exp(CHUNK * log_gamma)

    def step(r, dr):
        return chunk_decay[:, None, None] * r + dr, r

    _, r_prev = lax.scan(step, jnp.zeros((bsz, nh, dk, dv), F32), jnp.moveaxis(d_r, 1, 0))
    r_prev = jnp.moveaxis(r_prev, 0, 1)
    w_q = jnp.exp((idx + 1.0)[:, None] * log_gamma)
    inter = w_q[None, None, :, :, None] * jnp.einsum('bcthk,bchkv->bcthv', q, r_prev)
    return (intra + inter).reshape(bsz, seq, nh, dv)


def hybrid_mixer(u, w_in, i_bias, f_bias, m_norm_w, r_norm_w, w_out):
    bsz, seq, _ = u.shape
    proj = (u @ w_in).astype(F32)
    mq, mk, mv, mi, mf, mo, rq, rk, rv, rg = jnp.split(proj, HYB_SPLITS, axis=-1)
    h_m = mlstm_chunkwise(mq.reshape(bsz, seq, MLSTM_HEADS, MLSTM_QK),
                          mk.reshape(bsz, seq, MLSTM_HEADS, MLSTM_QK),
                          mv.reshape(bsz, seq, MLSTM_HEADS, MLSTM_V),
                          mi + i_bias.astype(F32), mf + f_bias.astype(F32))
    out_m = jax.nn.sigmoid(mo) * headnorm(h_m, m_norm_w)
    h_r = retention_chunkwise(rotary(rq.reshape(bsz, seq, RET_HEADS, RET_QK)),
                              rotary(rk.reshape(bsz, seq, RET_HEADS, RET_QK)),
                              rv.reshape(bsz, seq, RET_HEADS, RET_V))
    out_r = jax.nn.silu(rg) * headnorm(h_r, r_norm_w)
    cat = jnp.concatenate([out_m, out_r], axis=-1).astype(u.dtype)
    return cat @ w_out


def causal_depthwise_conv(x, w, b):
    out = lax.conv_general_dilated(x, w.astype(x.dtype)[:, None, :], window_strides=(1,),
                                   padding=[(SSD_CONV - 1, 0)],
                                   dimension_numbers=('NWC', 'WIO', 'NWC'),
                                   feature_group_count=x.shape[-1])
    return out + b.astype(x.dtype)


def ssd_chunked(x, dt, a, bm, cm):
    bsz, seq, nh, hp = x.shape
    ng, ns = bm.shape[2], bm.shape[3]
    rep = nh // ng
    nc = seq // CHUNK
    x = x.reshape(bsz, nc, CHUNK, ng, rep, hp)
    dt = dt.reshape(bsz, nc, CHUNK, ng, rep)
    bm = bm.reshape(bsz, nc, CHUNK, ng, ns)
    cm = cm.reshape(bsz, nc, CHUNK, ng, ns)
    cs = jnp.cumsum(dt * a.reshape(ng, rep), axis=2)
    mask = causal_mask()[None, None, :, :, None, None]
    seg = jnp.exp(jnp.where(mask, cs[:, :, :, None] - cs[:, :, None, :], -jnp.inf))
    cb = jnp.einsum('bctgn,bcsgn->bctsg', cm, bm)
    m = cb[..., None] * seg * dt[:, :, None]
    y_intra = jnp.einsum('bctsgr,bcsgrp->bctgrp', m, x)
    to_end = jnp.exp(cs[:, :, -1:] - cs) * dt
    states = jnp.einsum('bclgn,bclgr,bclgrp->bcgrpn', bm, to_end, x)
    chunk_decay = jnp.exp(cs[:, :, -1])

    def step(h, inp):
        dec, st = inp
        return dec[..., None, None] * h + st, h

    _, h_prev = lax.scan(step, jnp.zeros((bsz, ng, rep, hp, ns), F32),
                         (jnp.moveaxis(chunk_decay, 1, 0), jnp.moveaxis(states, 1, 0)))
    h_prev = jnp.moveaxis(h_prev, 0, 1)
    y_inter = jnp.einsum('bctgn,bcgrpn->bctgrp', cm, h_prev) * jnp.exp(cs)[..., None]
    return (y_intra + y_inter).reshape(bsz, seq, nh, hp)


def ssd_mixer(u, w_in, conv_w, conv_b, dt_bias, a_log, d_skip, norm_w, w_out):
    bsz, seq, _ = u.shape
    zxbcdt = u @ w_in
    z, xbc, dt = jnp.split(zxbcdt, (SSD_INNER, SSD_INNER + SSD_CONV_DIM), axis=-1)
    xbc = jax.nn.silu(causal_depthwise_conv(xbc, conv_w, conv_b).astype(F32))
    xs, bm, cm = jnp.split(xbc, (SSD_INNER, SSD_INNER + SSD_GROUPS * SSD_STATE), axis=-1)
    xs = xs.reshape(bsz, seq, SSD_HEADS, SSD_HEADDIM)
    bm = bm.reshape(bsz, seq, SSD_GROUPS, SSD_STATE)
    cm = cm.reshape(bsz, seq, SSD_GROUPS, SSD_STATE)
    dt = jax.nn.softplus(dt.astype(F32) + dt_bias.astype(F32))
    a = -jnp.exp(a_log.astype(F32))
    y = ssd_chunked(xs, dt, a, bm, cm) + d_skip.astype(F32)[:, None] * xs
    yg = (y.reshape(bsz, seq, SSD_INNER) * jax.nn.silu(z.astype(F32))).reshape(bsz, seq, SSD_GROUPS, -1)
    yg = yg * lax.rsqrt(jnp.mean(yg * yg, axis=-1, keepdims=True) + 1e-5)
    yg = yg.reshape(bsz, seq, SSD_INNER) * norm_w.astype(F32)
    return yg.astype(u.dtype) @ w_out


def squared_relu_mlp(u, w1, w2):
    h = jax.nn.relu(u @ w1)
    return (h * h) @ w2


def setup_inputs(seed: int = 0) -> dict:
    key = jax.random.key(seed)
    ks = jax.random.split(key, 20)
    n_even = (DEPTH + 1) // 2
    n_odd = DEPTH // 2
    nrm = jax.random.normal
    dt = jnp.exp(jax.random.uniform(ks[11], (n_odd, SSD_HEADS)) * (math.log(0.1) - math.log(0.001)) + math.log(0.001))
    dt = jnp.maximum(dt, 1e-4)
    return {
        "x": nrm(ks[0], (BATCH, SEQ, D_MODEL), F32),
        "norm_mix_w": 1.0 + 0.02 * nrm(ks[1], (DEPTH, D_MODEL), F32),
        "norm_mlp_w": 1.0 + 0.02 * nrm(ks[2], (DEPTH, D_MODEL), F32),
        "hyb_w_in": nrm(ks[3], (n_even, D_MODEL, HYB_PROJ), F32) * D_MODEL ** -0.5,
        "mlstm_i_bias": 0.1 * nrm(ks[4], (n_even, MLSTM_HEADS), F32),
        "mlstm_f_bias": jnp.linspace(3.0, 6.0, MLSTM_HEADS, dtype=F32)[None, :] + 0.1 * nrm(ks[5], (n_even, MLSTM_HEADS), F32),
        "mlstm_norm_w": 1.0 + 0.02 * nrm(ks[6], (n_even, MLSTM_INNER), F32),
        "ret_norm_w": 1.0 + 0.02 * nrm(ks[7], (n_even, RET_INNER), F32),
        "hyb_w_out": nrm(ks[8], (n_even, HYB_INNER, D_MODEL), F32) * HYB_INNER ** -0.5,
        "ssd_w_in": nrm(ks[9], (n_odd, D_MODEL, SSD_PROJ), F32) * D_MODEL ** -0.5,
        "ssd_conv_w": nrm(ks[10], (n_odd, SSD_CONV, SSD_CONV_DIM), F32) * SSD_CONV ** -0.5,
        "ssd_conv_b": 0.02 * nrm(ks[12], (n_odd, SSD_CONV_DIM), F32),
        "ssd_dt_bias": dt + jnp.log(-jnp.expm1(-dt)),
        "ssd_a_log": jnp.log(jax.random.uniform(ks[13], (n_odd, SSD_HEADS), F32, 1.0, 16.0)),
        "ssd_d": 1.0 + 0.1 * nrm(ks[14], (n_odd, SSD_HEADS), F32),
        "ssd_norm_w": 1.0 + 0.02 * nrm(ks[15], (n_odd, SSD_INNER), F32),
        "ssd_w_out": nrm(ks[16], (n_odd, SSD_INNER, D_MODEL), F32) * SSD_INNER ** -0.5,
        "mlp_w1": nrm(ks[17], (DEPTH, D_MODEL, D_FF), F32) * D_MODEL ** -0.5,
        "mlp_w2": nrm(ks[18], (DEPTH, D_FF, D_MODEL), F32) * D_FF ** -0.5,
        "final_norm_w": 1.0 + 0.02 * nrm(ks[19], (D_MODEL,), F32),
    }


def reference(x, norm_mix_w, norm_mlp_w, hyb_w_in, mlstm_i_bias, mlstm_f_bias, mlstm_norm_w, ret_norm_w,
              hyb_w_out, ssd_w_in, ssd_conv_w, ssd_conv_b, ssd_dt_bias, ssd_a_log, ssd_d, ssd_norm_w,
              ssd_w_out, mlp_w1, mlp_w2, final_norm_w):
    h = x
    for layer in range(DEPTH):
        u = rmsnorm(h, norm_mix_w[layer])
        j = layer // 2
        if layer % 2 == 0:
            mix = hybrid_mixer(u, hyb_w_in[j], mlstm_i_bias[j], mlstm_f_bias[j], mlstm_norm_w[j],
                               ret_norm_w[j], hyb_w_out[j])
        else:
            mix = ssd_mixer(u, ssd_w_in[j], ssd_conv_w[j], ssd_conv_b[j], ssd_dt_bias[j], ssd_a_log[j],
                            ssd_d[j], ssd_norm_w[j], ssd_w_out[j])
        h = h + mix.astype(h.dtype)
        h = h + squared_relu_mlp(rmsnorm(h, norm_mlp_w[layer]), mlp_w1[layer], mlp_w2[layer]).astype(h.dtype)
    return rmsnorm(h, final_norm_w)
```

```python
import contextlib
import math
import numpy as np
import concourse.bass as bass
import concourse.mybir as mybir
from concourse.bass_utils import run_bass_kernel_spmd

F32 = mybir.dt.float32
BF16 = mybir.dt.bfloat16
ALU = mybir.AluOpType
AF = mybir.ActivationFunctionType
AX = mybir.AxisListType

D = 1024
L = 128
DFF = 4096
HYB_PROJ = 3080
SSD_PROJ = 6176
NEG = -30000.0
import os
HYB_STOP = int(os.environ.get('HYB_STOP', '99'))
SSD_STOP = int(os.environ.get('SSD_STOP', '99'))
SSD_SUB = os.environ.get('SSD_SUB', 'z')
ENGS = ("pe", "act", "dve", "pool", "sp")


class Buf:
    __slots__ = ("name", "writer", "readers", "excl")

    def __init__(self, name="", excl=False):
        self.name = name
        self.writer = None
        self.readers = []
        self.excl = excl


class V:
    __slots__ = ("ap", "bufs")

    def __init__(self, ap, bufs):
        self.ap = ap
        self.bufs = tuple(bufs)

    def __getitem__(self, idx):
        return V(self.ap[idx], self.bufs)

    def bc(self, shape):
        return V(self.ap.broadcast_to(list(shape)), self.bufs)

    def re(self, s, **kw):
        return V(self.ap.rearrange(s, **kw), self.bufs)


def _ap(x):
    return x.ap if isinstance(x, V) else x


def _bufs(*xs):
    out = []
    for x in xs:
        if isinstance(x, V):
            out.extend(x.bufs)
    return out


class Op:
    __slots__ = ("eng", "fn", "deps", "is_dma", "sem", "sigval", "needs_sig", "idx")

    def __init__(self, eng, fn, is_dma, sem):
        self.eng = eng
        self.fn = fn
        self.deps = []
        self.is_dma = is_dma
        self.sem = sem
        self.sigval = None
        self.needs_sig = False


class Prog:
    def __init__(self, nc):
        self.nc = nc
        self.ops = []
        self.phase = 0
        self.last_on = {}
        self.dmas_since = []
        self.gate = None
        self.gate_passed = set()

    def barrier(self):
        deps = list(self.last_on.values()) + list(self.dmas_since)
        self.gate = deps
        self.gate_passed = set()
        self.dmas_since = []
        self.phase += 1

    def op(self, eng, fn, reads=(), writes=(), dma_sem=None, skip_waw=False):
        is_dma = dma_sem is not None
        sem = ("dma", dma_sem, self.phase) if is_dma else ("eng", eng, self.phase)
        o = Op(eng, fn, is_dma, sem)
        o.idx = len(self.ops)
        deps = []
        for b in reads:
            if b.writer is not None:
                deps.append(b.writer)
            if b.excl:
                deps.extend(r for r in b.readers if r.eng != eng)
        for b in writes:
            if b.writer is not None and not skip_waw:
                deps.append(b.writer)
            deps.extend(b.readers)
        if self.gate is not None and eng not in self.gate_passed:
            deps.extend(self.gate)
            self.gate_passed.add(eng)
        latest = {}
        seen = set()
        for d in deps:
            if d is o or id(d) in seen:
                continue
            seen.add(id(d))
            if d.eng == "pe" and eng == "pe" and not d.is_dma and not is_dma:
                continue
            if d.is_dma:
                o.deps.append(d)
                d.needs_sig = True
            else:
                cur = latest.get(d.sem)
                if cur is None or d.idx > cur.idx:
                    latest[d.sem] = d
        for d in latest.values():
            o.deps.append(d)
            d.needs_sig = True
        for b in reads:
            b.readers.append(o)
        for b in writes:
            b.writer = o
            b.readers = []
        self.ops.append(o)
        self.last_on[eng] = o
        if is_dma:
            self.dmas_since.append(o)
            o.needs_sig = True
        return o

    def mm(self, out, lhsT, rhs, start=True, stop=True):
        return self.op("pe", lambda e: e.matmul(_ap(out), lhsT=_ap(lhsT), rhs=_ap(rhs), start=start, stop=stop),
                       reads=_bufs(lhsT, rhs), writes=_bufs(out))

    def tr(self, out, in_, ident):
        return self.op("pe", lambda e: e.transpose(_ap(out), _ap(in_), _ap(ident)),
                       reads=_bufs(in_, ident), writes=_bufs(out))

    def act(self, out, in_, func, bias=None, scale=None, accum=None):
        kw = {}
        if bias is not None:
            kw["bias"] = _ap(bias)
        if scale is not None:
            kw["scale"] = _ap(scale)
        if accum is not None:
            kw["accum_out"] = _ap(accum)
        return self.op("act", lambda e: e.activation(out=_ap(out), in_=_ap(in_), func=func, **kw),
                       reads=_bufs(in_, bias, scale), writes=_bufs(out, accum))

    def tt(self, eng, out, in0, in1, op):
        return self.op(eng, lambda e: e.tensor_tensor(out=_ap(out), in0=_ap(in0), in1=_ap(in1), op=op),
                       reads=_bufs(in0, in1), writes=_bufs(out))

    def ts(self, eng, out, in0, s1, op0, s2=None, op1=None):
        if op1 is None:
            return self.op(eng, lambda e: e.tensor_scalar(out=_ap(out), in0=_ap(in0), scalar1=_ap(s1), scalar2=None, op0=op0),
                           reads=_bufs(in0, s1), writes=_bufs(out))
        return self.op(eng, lambda e: e.tensor_scalar(out=_ap(out), in0=_ap(in0), scalar1=_ap(s1), scalar2=_ap(s2),
                                                      op0=op0, op1=op1),
                       reads=_bufs(in0, s1, s2), writes=_bufs(out))

    def stt(self, eng, out, in0, scalar, in1, op0, op1):
        return self.op(eng, lambda e: e.scalar_tensor_tensor(out=_ap(out), in0=_ap(in0), scalar=_ap(scalar), in1=_ap(in1),
                                                             op0=op0, op1=op1),
                       reads=_bufs(in0, scalar, in1), writes=_bufs(out))

    def cp(self, eng, out, in_):
        if eng == "act":
            return self.op("act", lambda e: e.copy(out=_ap(out), in_=_ap(in_)), reads=_bufs(in_), writes=_bufs(out))
        return self.op(eng, lambda e: e.tensor_copy(out=_ap(out), in_=_ap(in_)), reads=_bufs(in_), writes=_bufs(out))

    def red(self, eng, out, in_, op):
        return self.op(eng, lambda e: e.tensor_reduce(out=_ap(out), in_=_ap(in_), axis=AX.X, op=op),
                       reads=_bufs(in_), writes=_bufs(out))

    def memset(self, eng, out, val):
        return self.op(eng, lambda e: e.memset(_ap(out), val), writes=_bufs(out))

    def recip(self, out, in_):
        return self.op("dve", lambda e: e.reciprocal(out=_ap(out), in_=_ap(in_)), reads=_bufs(in_), writes=_bufs(out))

    def dma(self, q, out, in_, sem, skip_waw=False):
        return self.op(q, lambda e: e.dma_start(out=_ap(out), in_=_ap(in_)), reads=_bufs(in_), writes=_bufs(out),
                       dma_sem=sem, skip_waw=skip_waw)

    def emit(self, final_wait_ops=()):
        nc = self.nc
        counters = {}
        for o in final_wait_ops:
            o.needs_sig = True
        for o in self.ops:
            if o.needs_sig:
                inc = 16 if o.is_dma else 1
                counters[o.sem] = counters.get(o.sem, 0) + inc
                o.sigval = counters[o.sem]
        self.max_sem = counters
        with contextlib.ExitStack() as st:
            sems = {}
            for i, k in enumerate(counters.keys()):
                sems[k] = st.enter_context(nc.semaphore("s%d" % i))
            block = st.enter_context(nc.Block())
            per_eng = {e: [o for o in self.ops if o.eng == e] for e in ENGS}

            def make_body(e):
                def body(eng):
                    waited = {}
                    for o in per_eng[e]:
                        need = {}
                        for d in o.deps:
                            if need.get(d.sem, 0) < d.sigval:
                                need[d.sem] = d.sigval
                        for k, v in need.items():
                            if waited.get(k, 0) >= v:
                                continue
                            eng.wait_ge(sems[k], v)
                            waited[k] = v
                        ins = o.fn(eng)
                        if o.needs_sig:
                            ins.then_inc(sems[o.sem], 16 if o.is_dma else 1)
                    if e == "sp":
                        fin = {}
                        for o in final_wait_ops:
                            fin[o.sem] = max(fin.get(o.sem, 0), o.sigval)
                        for k, v in fin.items():
                            eng.wait_ge(sems[k], v)
                return body

            regs = {"pe": block.tensor, "act": block.scalar, "dve": block.vector,
                    "pool": block.gpsimd, "sp": block.sync}
            for e in ENGS:
                regs[e](make_body(e))


class Arena:
    def __init__(self, tensor, nbytes):
        self.t = tensor
        self.nbytes = nbytes
        self.off = 0
        self.base = 0

    def reset(self):
        self.off = self.base

    def alloc(self, shape, dt, name="", nbufs=None):
        n = 1
        for s in shape[1:]:
            n *= s
        esz = 2 if dt == BF16 else 4
        nb = (n * esz + 31) // 32 * 32
        assert self.off + nb <= self.nbytes, "arena overflow %s: %d + %d > %d" % (name, self.off, nb, self.nbytes)
        o4 = self.off // 4
        ap = self.t[:, o4:o4 + nb // 4]
        if dt == BF16:
            ap = ap.bitcast(BF16)
        ap = ap[0:shape[0], 0:n]
        if len(shape) == 3:
            ap = ap.rearrange("p (a b) -> p a b", b=shape[2])
        elif len(shape) == 4:
            ap = ap.rearrange("p (a b c) -> p a b c", b=shape[2], c=shape[3])
        self.off += nb
        return V(ap, [Buf(name)])


class Psum:
    def __init__(self, tensor):
        self.t = tensor
        self.bufs = {}

    def new_phase(self):
        self.bufs = {}

    def view(self, bank, col0, ncols, dt=F32, shape=None, sub=0, parts=128):
        key = (bank, 0)
        if key not in self.bufs:
            self.bufs[key] = Buf("ps%d_%s" % (bank, sub), excl=True)
        ap = self.t[:, bank * 512 + col0: bank * 512 + col0 + ncols]
        if dt == BF16:
            ap = ap.bitcast(BF16)
        ap = ap[0:parts]
        if shape is not None:
            if len(shape) == 2:
                ap = ap.rearrange("p (a b) -> p a b", b=shape[1])
        return V(ap, [self.bufs[key]])


CST = {}


def _build_consts():
    c = {}
    idx = np.arange(128)
    p = idx[:, None]
    j = idx[None, :]
    c["identf"] = (p == j).astype(np.float32)
    c["triNeg"] = -(p <= j).astype(np.float32)
    c["triPos"] = (p <= j).astype(np.float32)
    c["maskA"] = np.where(j <= p, 0.0, NEG).astype(np.float32)
    c["maskB"] = np.where(j >= p, 0.0, NEG).astype(np.float32)
    c["mask01B"] = (j >= p).astype(np.float32)
    c["e127"] = np.zeros((128, 128), np.float32)
    c["e127"][127, :] = 1.0
    lg = np.log(1.0 - 2.0 ** (-5.0 - np.arange(4, dtype=np.float64)))
    rel = (j - p).astype(np.float64)
    dec = np.where((j >= p)[:, None, :], np.exp(np.maximum(rel, 0.0)[:, None, :] * lg[None, :, None]), 0.0)
    c["decayT"] = dec.reshape(128, 512).astype(np.float32)
    c["wq8"] = (np.exp((idx[:, None] + 1.0) * lg[None, :]) * 0.125).astype(np.float32)
    c["wk"] = np.exp((127.0 - idx[:, None]) * lg[None, :]).astype(np.float32)
    c["cd"] = np.broadcast_to(np.exp(128.0 * lg)[None, :], (128, 4)).astype(np.float32)
    c["ones"] = np.ones((128, 8), np.float32)
    order = ["identf", "triNeg", "triPos", "maskA", "maskB", "mask01B", "e127", "decayT", "wq8", "wk", "cd", "ones"]
    offs = {}
    o = 0
    for k in order:
        offs[k] = (o, c[k].shape[1])
        o += c[k].shape[1]
    return np.concatenate([c[k] for k in order], axis=1), offs


def _rope_tables(S):
    inv = 10000.0 ** (-np.arange(0, 64, 2, dtype=np.float32) / np.float32(64))
    ang = np.arange(S, dtype=np.float32)[:, None] * inv[None, :].astype(np.float32)
    ang = ang.astype(np.float32).astype(np.float64)
    cos = np.cos(ang).astype(np.float32).reshape(S // 128, 128, 32).transpose(1, 0, 2)
    sin = np.sin(ang).astype(np.float32).reshape(S // 128, 128, 32).transpose(1, 0, 2)
    return np.ascontiguousarray(np.concatenate([cos.reshape(128, -1), sin.reshape(128, -1)], axis=1))


def build_program(S, plan, final_norm=True):
    NCH = S // L
    nc = bass.Bass("TRN2", target_bir_lowering=False)
    cst_np, coffs = _build_consts()
    NCST = cst_np.shape[1]

    def din(name, shape):
        return nc.dram_tensor(name, list(shape), F32, kind="ExternalInput").ap()

    x_d = din("x", [S, D])
    hyb_w_in = din("hyb_w_in", [2, D, HYB_PROJ])
    hyb_w_out = din("hyb_w_out", [2, D, D])
    ssd_w_in = din("ssd_w_in", [2, D, SSD_PROJ])
    ssd_w_out = din("ssd_w_out", [2, 2048, D])
    mlp_w1 = din("mlp_w1", [4, D, DFF])
    mlp_w2 = din("mlp_w2", [4, DFF, D])
    cst_d = din("cst", [128, NCST])
    rope_d = din("rope", [128, 2 * NCH * 32])
    nmix_d = din("nmix_col", [4, 128, 8])
    nmlp_d = din("nmlp_col", [4, 128, 8])
    fnw_d = din("fnw_b", [128, D])
    hybp_d = din("hyb_small", [2, 128, 8 + 1024])
    ssdp_d = din("ssd_small", [2, 128, 128 + 32 + 96 + 16])
    out_d = nc.dram_tensor("out", [S, D], F32, kind="ExternalOutput").ap()
    hbuf = nc.dram_tensor("hbuf", [S, D], F32).ap()

    ARENA_BYTES = 211968
    with contextlib.ExitStack() as st:
        arena_t = st.enter_context(nc.sbuf_tensor("arena", [128, ARENA_BYTES // 4], F32))
        psum_t = st.enter_context(nc.psum_tensor("psum", [128, 4096], F32))
        P = Prog(nc)
        A = Arena(arena_t, ARENA_BYTES)
        PS = Psum(psum_t)

        identf = A.alloc([128, 128], F32, "identf")
        identb = A.alloc([128, 128], BF16, "identb")
        P.dma("act", identf, cst_d[:, coffs["identf"][0]:coffs["identf"][0] + 128], "cid")
        P.cp("dve", identb, identf)
        A.base = A.off

        cbuf = [None]

        def small_load(shape, src_ap, name):
            t = A.alloc(shape, F32, name)
            t = V(t.ap, [cbuf[0]])
            P.dma("act", t, src_ap, "c", skip_waw=True)
            return t

        def cst_load(name, dt=F32):
            o, n = coffs[name]
            t = small_load([128, n], cst_d[:, o:o + n], name)
            if dt == BF16:
                tb = A.alloc([128, n], BF16, name + "b")
                P.cp("pool", tb, t)
                return tb
            return t

        out_stores = []
        src = x_d
        for pi, (kind, layer) in enumerate(plan):
            last = pi == len(plan) - 1
            dst = out_d if last else hbuf
            A.reset()
            PS.new_phase()
            cbuf[0] = Buf("consts%d" % pi)
            if kind == "mlp":
                _phase_mlp(P, A, PS, nc, layer, src, dst, S, mlp_w1, mlp_w2, nmlp_d, fnw_d, small_load, identb,
                           final_norm and last, out_stores if last else None)
            elif kind == "hyb":
                _phase_hyb(P, A, PS, nc, layer, src, dst, S, hyb_w_in, hyb_w_out, nmix_d, hybp_d, rope_d,
                           small_load, cst_load, identb, identf, out_stores if last else None)
            else:
                _phase_ssd(P, A, PS, nc, layer, src, dst, S, ssd_w_in, ssd_w_out, nmix_d, ssdp_d,
                           small_load, cst_load, identb, identf, out_stores if last else None)
            P.barrier()
            src = hbuf
        P.emit(final_wait_ops=out_stores)
    return nc


def _rms_scale(P, ssq, ms, ln, rstd, n, eps):
    P.ts("dve", ms, ssq, 1.0 / n, ALU.mult, eps, ALU.add)
    P.act(ln, ms, AF.Ln)
    P.act(rstd, ln, AF.Exp, scale=-0.5)


def _fold(P, k, wv, col):
    if k % 2:
        P.act(wv, wv, AF.Identity, scale=col)
    else:
        P.ts("dve", wv, wv, col, ALU.mult)


def _load_weight(P, wbf, wd, ktiles, per, sem):
    wv = wd.rearrange("(k p) n -> p k n", p=128)
    for k0 in range(0, ktiles, per):
        P.dma("pool", wbf[:, k0:k0 + per, :], wv[:, k0:k0 + per, :], sem, skip_waw=True)


def _phase_mlp(P, A, PS, nc, layer, src, dst, S, w1_d, w2_d, nmlp_d, fnw_d, small_load, identb, final, out_stores):
    TT = 512
    NT = S // TT
    w1 = A.alloc([128, 8, DFF], BF16, "w1")
    w2 = A.alloc([128, 32, D], BF16, "w2")
    nwc = small_load([128, 8], nmlp_d[layer], "nwc")
    _load_weight(P, w1, w1_d[layer], 8, 1, "w1")
    _load_weight(P, w2, w2_d[layer], 32, 4, "w2")
    for k in range(8):
        _fold(P, k, w1[:, k, :], nwc[:, k:k + 1])
    fnw = None
    if final:
        fnw = small_load([128, D], fnw_d, "fnw")
    ht = [A.alloc([128, D], F32, "ht%d" % j) for j in range(4)]
    u = [A.alloc([128, D], BF16, "u%d" % i) for i in range(2)]
    uTj = [None] * 4
    uT_all = A.alloc([128, 8, TT], BF16, "uT")
    ubufs = [Buf("uT%d" % j) for j in range(4)]
    for j in range(4):
        uTj[j] = V(uT_all.ap[:, :, j * 128:(j + 1) * 128], [ubufs[j]])
    uT = V(uT_all.ap, ubufs)
    h1T = A.alloc([128, 32, TT], BF16, "h1T")
    tmp = [A.alloc([128, TT], F32, "tmp%d" % i) for i in range(2)]
    junk = A.alloc([128, D], BF16, "junk")
    stt = [A.alloc([128, 8], F32, "io%d" % j) for j in range(4)]
    pT = [PS.view(b, 0, 512, BF16) for b in (0, 1)]
    pm1 = [PS.view(b, 0, 512) for b in (2, 3)]
    pm2 = [PS.view(b, 0, 512) for b in (4, 5, 6, 7)]

    for T in range(NT):
        for j in range(4):
            r0 = T * TT + j * 128
            P.dma("sp", ht[j], src[r0:r0 + 128, :], "io%d" % j)
            s = stt[j]
            P.act(junk, ht[j], AF.Square, accum=s[:, 0:1])
            _rms_scale(P, s[:, 0:1], s[:, 1:2], s[:, 2:3], s[:, 3:4], D, 1e-6)
            uu = u[j % 2]
            P.ts("dve", uu, ht[j], s[:, 3:4], ALU.mult)
            for k in range(8):
                P.tr(pT[j % 2][:, k * 128:(k + 1) * 128], uu[:, k * 128:(k + 1) * 128], identb)
            P.cp("dve" if j % 2 else "act", uTj[j], pT[j % 2].re("p (a b) -> p a b", b=128))
        for f in range(32):
            pb = pm1[f % 2]
            for k in range(8):
                P.mm(pb, w1[:, k, f * 128:(f + 1) * 128], uT[:, k, :], start=(k == 0), stop=(k == 7))
            P.act(tmp[f % 2], pb, AF.Relu)
            P.tt("pool" if f % 3 else "dve", h1T[:, f, :], tmp[f % 2], tmp[f % 2], ALU.mult)
        for t in range(4):
            for n in range(2):
                pb = pm2[(t % 2) * 2 + n]
                for f in range(32):
                    P.mm(pb, h1T[:, f, t * 128:(t + 1) * 128], w2[:, f, n * 512:(n + 1) * 512],
                         start=(f == 0), stop=(f == 31))
                P.tt("dve", ht[t][:, n * 512:(n + 1) * 512], pb, ht[t][:, n * 512:(n + 1) * 512], ALU.add)
            r0 = T * TT + t * 128
            if final:
                s = stt[t]
                P.act(junk, ht[t], AF.Square, accum=s[:, 4:5])
                _rms_scale(P, s[:, 4:5], s[:, 5:6], s[:, 6:7], s[:, 7:8], D, 1e-6)
                P.stt("dve", ht[t], ht[t], s[:, 7:8], fnw, ALU.mult, ALU.mult)
            o = P.dma("sp", dst[r0:r0 + 128, :], ht[t], "io%d" % t)
            if out_stores is not None:
                out_stores.append(o)


def _phase_hyb(P, A, PS, nc, layer, src, dst, S, w_in_d, w_out_d, nmix_d, hybp_d, rope_d, small_load, cst_load,
               identb, identf, out_stores):
    j = layer // 2
    NCH = S // L
    w_in = A.alloc([128, 8, HYB_PROJ], BF16, "w_in")
    w_out = A.alloc([128, 8, D], BF16, "w_out")
    nwc = small_load([128, 8], nmix_d[layer], "nwc")
    _load_weight(P, w_in, w_in_d[j], 8, 1, "w1")
    _load_weight(P, w_out, w_out_d[j], 8, 4, "w2")
    for k in range(8):
        _fold(P, k, w_in[:, k, :], nwc[:, k:k + 1])
    triNeg = cst_load("triNeg")
    maskA = cst_load("maskA")
    maskB = cst_load("maskB")
    e127 = cst_load("e127")
    decayT = cst_load("decayT")
    wq8 = cst_load("wq8")
    wk = cst_load("wk")
    cd = cst_load("cd")
    onesb = cst_load("ones", BF16)
    rope = small_load([128, 2 * NCH * 32], rope_d, "rope")
    cosT = rope[:, 0:NCH * 32].re("p (c f) -> p c f", f=32)
    sinT = rope[:, NCH * 32:2 * NCH * 32].re("p (c f) -> p c f", f=32)
    small = small_load([128, 8 + 1024], hybp_d[j], "hybsmall")
    bias8 = small[:, 0:8]
    mnw = small[:, 8:8 + 512]
    rnw = small[:, 8 + 512:8 + 1024]

    hc = [A.alloc([128, D], F32, "hc%d" % i) for i in range(2)]
    u = A.alloc([128, D], BF16, "u")
    uT = A.alloc([128, 8, 128], BF16, "uT")
    junk = A.alloc([128, D], BF16, "junk")
    proj_l = [A.alloc([128, HYB_PROJ], F32, "proj%d" % i) for i in range(2)]
    stt = A.alloc([128, 8], F32, "stt")
    def g(n, name):
        return A.alloc([128, n], F32, name)
    gif = g(8, "gif")
    ee = g(4, "ee")
    lsp = g(4, "lsp")
    av = g(4, "av")
    gm = g(8, "gm")
    cm = g(4, "cm")
    negM = g(4, "negM")
    bM = g(4, "bM")
    sel = g(8, "sel")
    mprev = g(4, "mprev")
    ea = g(8, "ea")
    eb = g(8, "eb")
    eo1 = g(8, "eo1")
    eo2 = g(8, "eo2")
    sci, emt = eo1[:, 0:4], eo1[:, 4:8]
    w2, sold = eo2[:, 0:4], eo2[:, 4:8]
    tmpA = A.alloc([128, 4, 128], F32, "tmpA")
    Dt = A.alloc([128, 4, 128], F32, "Dt")
    q8m = A.alloc([128, 256], BF16, "q8m")
    qsm = A.alloc([128, 256], BF16, "qsm")
    km = A.alloc([128, 256], BF16, "km")
    kwm = A.alloc([128, 256], BF16, "kwm")
    vm = A.alloc([128, 512], BF16, "vm")
    rq = A.alloc([128, 256], F32, "rq")
    rk = A.alloc([128, 256], F32, "rk")
    rt = [A.alloc([128, 128], F32, "rt%d" % i) for i in range(4)]
    q8r = A.alloc([128, 256], BF16, "q8r")
    qsr = A.alloc([128, 256], BF16, "qsr")
    kr = A.alloc([128, 256], BF16, "kr")
    kwr = A.alloc([128, 256], BF16, "kwr")
    vr = A.alloc([128, 512], BF16, "vr")
    kTm = A.alloc([128, 2, 128], BF16, "kTm")
    kTr = A.alloc([128, 2, 128], BF16, "kTr")
    qTm = A.alloc([128, 4, 128], BF16, "qTm")
    qTr = A.alloc([128, 4, 128], BF16, "qTr")
    qsTm = A.alloc([128, 4, 128], BF16, "qsTm")
    qsTr = A.alloc([128, 4, 128], BF16, "qsTr")
    St_m = A.alloc([128, 4, 128], BF16, "St_m")
    St_r = A.alloc([128, 4, 128], BF16, "St_r")
    hm = A.alloc([128, 4, 128], F32, "hm")
    hr = A.alloc([128, 4, 128], F32, "hr")
    sq = A.alloc([128, 4, 128], F32, "sq")
    d1 = g(4, "d1")
    d2 = g(4, "d2")
    rec = g(4, "rec")
    nst = A.alloc([128, 24], F32, "nst")
    nsr = A.alloc([128, 24], F32, "nsr")
    gw_m = A.alloc([128, 512], F32, "gw_m")
    gw_r = A.alloc([128, 512], F32, "gw_r")
    cat = A.alloc([128, D], BF16, "cat")
    catT = A.alloc([128, 8, 128], BF16, "catT")
    Cst = A.alloc([128, 2, 256], F32, "Cst")
    nstt = A.alloc([128, 2], F32, "nstt")
    Cbf = A.alloc([128, 2, 256], BF16, "Cbf")
    nbf = A.alloc([128, 2], BF16, "nbf")
    Rst = A.alloc([128, 2, 256], F32, "Rst")
    Rbf = A.alloc([128, 2, 256], BF16, "Rbf")

    pp = [PS.view(b, 0, 512) for b in (0, 1)]
    pT = PS.view(2, 0, 512, BF16)
    pTq = PS.view(6, 0, 256, BF16)
    pbc = PS.view(3, 0, 512, shape=[4, 128])
    pS_m = PS.view(4, 0, 512, shape=[4, 128])
    pS_r = PS.view(5, 0, 512, shape=[4, 128])
    pdC = PS.view(3, 0, 512, shape=[2, 256])
    pdR = PS.view(6, 0, 512, shape=[2, 256])
    p_b = PS.view(7, 0, 4, sub="b")
    p_sel = PS.view(7, 8, 8, sub="sel")
    p_den = PS.view(7, 16, 4, sub="den")
    p_dn = PS.view(7, 24, 2, sub="dn")

    for t_ in (Cst, nstt, Rst, mprev):
        P.memset("pool", t_, 0.0)
    for t_ in (Cbf, nbf, Rbf, qTm, qTr, qsTm, qsTr):
        P.memset("pool", t_, 0.0)

    blocks = [(b * 512, min(512, HYB_PROJ - b * 512)) for b in range(7)]
    def _front(c):
        h = hc[c % 2]
        r0 = c * L
        proj = proj_l[c % 2]
        P.dma("sp", h, src[r0:r0 + L, :], "io%d" % (c % 2))
        P.act(junk, h, AF.Square, accum=stt[:, 0:1])
        _rms_scale(P, stt[:, 0:1], stt[:, 1:2], stt[:, 2:3], stt[:, 3:4], D, 1e-6)
        P.ts("dve", u, h, stt[:, 3:4], ALU.mult)
        for k in range(8):
            P.tr(pT[:, k * 128:(k + 1) * 128], u[:, k * 128:(k + 1) * 128], identb)
        P.cp("act", uT, pT.re("p (a b) -> p a b", b=128))
        for bi, (c0, cn) in enumerate(blocks):
            pb = pp[bi % 2]
            for k in range(8):
                P.mm(pb[:, 0:cn], uT[:, k, :], w_in[:, k, c0:c0 + cn], start=(k == 0), stop=(k == 7))
            P.cp("dve" if bi % 2 else "act", proj[:, c0:c0 + cn], pb[:, 0:cn])
    def _core(c):
        h = hc[c % 2]
        r0 = c * L
        proj = proj_l[c % 2]
        P.tt("dve", gif, proj[:, 1024:1032], bias8, ALU.add)
        P.act(ee, gif[:, 4:8], AF.Exp, scale=-1.0)
        P.act(lsp, ee, AF.Ln, bias=1.0)
        P.mm(p_b, triNeg, lsp)
        P.cp("dve", gm[:, 0:4], p_b)
        P.tt("dve", av, gif[:, 0:4], p_b, ALU.subtract)
        for hh in range(4):
            P.mm(pbc[:, hh, :], av[:, hh:hh + 1].bc([128, 128]), identf)
        P.tt("dve", tmpA, pbc, maskA.re("p (o s) -> p o s", o=1).bc([128, 4, 128]), ALU.add)
        P.red("dve", cm, tmpA, ALU.max)
        P.tt("dve", gm[:, 4:8], cm, mprev, ALU.max)
        P.ts("dve", negM, gm[:, 4:8], -1.0, ALU.mult)
        for hh in range(4):
            P.mm(pbc[:, hh, :], negM[:, hh:hh + 1].bc([128, 128]), identf)
        P.mm(p_sel, e127, gm)
        for hh in range(4):
            P.stt("dve", Dt[:, hh, :], pbc[:, hh, :], av[:, hh:hh + 1], maskB, ALU.add, ALU.add)
        P.act(Dt, Dt, AF.Exp)
        P.tt("dve", ea[:, 0:4], mprev, gm[:, 4:8], ALU.subtract)
        P.tt("dve", ea[:, 4:8], negM, gm[:, 0:4], ALU.subtract)
        P.act(eo1, ea, AF.Exp)
        P.cp("dve", sel, p_sel)
        P.tt("dve", eb[:, 0:4], av, sel[:, 4:8], ALU.subtract)
        P.tt("dve", eb[:, 4:8], mprev, sel[:, 4:8], ALU.subtract)
        P.act(eo2, eb, AF.Exp)
        P.tt("dve", mprev, sel[:, 0:4], sel[:, 4:8], ALU.add)
        pq = proj[:, 0:256].re("p (h k) -> p h k", k=64)
        pk = proj[:, 256:512].re("p (h k) -> p h k", k=64)
        P.act(q8m, proj[:, 0:256], AF.Identity, scale=0.125)
        P.stt("dve", qsm.re("p (h k) -> p h k", k=64), pq, 0.125,
              sci.re("p (h o) -> p h o", o=1).bc([128, 4, 64]), ALU.mult, ALU.mult)
        P.cp("pool", km, proj[:, 256:512])
        P.tt("pool", kwm.re("p (h k) -> p h k", k=64), pk, w2.re("p (h o) -> p h o", o=1).bc([128, 4, 64]), ALU.mult)
        P.cp("pool", vm, proj[:, 512:1024])
        cosc = cosT[:, c, :].re("p (o f) -> p o f", o=1).bc([128, 4, 32])
        sinc = sinT[:, c, :].re("p (o f) -> p o f", o=1).bc([128, 4, 32])
        for (srcc, dstt) in ((1544, rq), (1800, rk)):
            sv = proj[:, srcc:srcc + 256].re("p (h two f) -> p h two f", two=2, f=32)
            dv = dstt.re("p (h two f) -> p h two f", two=2, f=32)
            t1 = sv[:, :, 0, :]
            t2 = sv[:, :, 1, :]
            ta = rt[0].re("p (h f) -> p h f", f=32)
            tb = rt[1].re("p (h f) -> p h f", f=32)
            tc = rt[2].re("p (h f) -> p h f", f=32)
            td = rt[3].re("p (h f) -> p h f", f=32)
            P.tt("pool", ta, t1, cosc, ALU.mult)
            P.tt("pool", tb, t2, sinc, ALU.mult)
            P.tt("pool", dv[:, :, 0, :], ta, tb, ALU.subtract)
            P.tt("pool", tc, t1, sinc, ALU.mult)
            P.tt("pool", td, t2, cosc, ALU.mult)
            P.tt("pool", dv[:, :, 1, :], tc, td, ALU.add)
        P.act(q8r, rq, AF.Identity, scale=0.125)
        P.tt("pool", qsr.re("p (h k) -> p h k", k=64), rq.re("p (h k) -> p h k", k=64),
             wq8.re("p (h o) -> p h o", o=1).bc([128, 4, 64]), ALU.mult)
        P.cp("pool", kr, rk)
        P.tt("pool", kwr.re("p (h k) -> p h k", k=64), rk.re("p (h k) -> p h k", k=64),
             wk.re("p (h o) -> p h o", o=1).bc([128, 4, 64]), ALU.mult)
        P.cp("pool", vr, proj[:, 2056:2568])
        tl = [q8m, km, q8r, kr, qsm, qsr]
        for i in range(4):
            for bb in range(2):
                P.tr(pT[:, (i * 2 + bb) * 128:(i * 2 + bb + 1) * 128], tl[i][:, bb * 128:(bb + 1) * 128], identb)
        for i in range(4, 6):
            for bb in range(2):
                P.tr(pTq[:, ((i - 4) * 2 + bb) * 128:((i - 4) * 2 + bb + 1) * 128], tl[i][:, bb * 128:(bb + 1) * 128], identb)
        pT3 = pT.re("p (a b) -> p a b", b=128)
        pTq3 = pTq.re("p (a b) -> p a b", b=128)

        def masked_evac(dstt, srcv, e0, e1):
            dv = dstt.re("p (b two) t -> p b two t", two=2)
            P.cp(e0, dv[0:64, :, 0, :], srcv[0:64])
            P.cp(e1, dv[64:128, :, 1, :], srcv[64:128])
        masked_evac(qTm, pT3[:, 0:2, :], "act", "dve")
        P.cp("act", kTm, pT3[:, 2:4, :])
        masked_evac(qTr, pT3[:, 4:6, :], "act", "dve")
        P.cp("dve", kTr, pT3[:, 6:8, :])
        masked_evac(qsTm, pTq3[:, 0:2, :], "act", "dve")
        masked_evac(qsTr, pTq3[:, 2:4, :], "act", "dve")

        def stv(t_, hh):
            return t_[(hh % 2) * 64:(hh % 2) * 64 + 64, hh // 2, (hh % 2) * 128:(hh % 2) * 128 + 128]

        def stc(t_, hh):
            return t_[:, hh // 2, (hh % 2) * 128:(hh % 2) * 128 + 128]

        def nv_(t_, hh):
            return t_[(hh % 2) * 64:(hh % 2) * 64 + 64, hh // 2:hh // 2 + 1]
        for hh in range(4):
            P.mm(pS_m[:, hh, :], kTm[:, hh // 2, :], qTm[:, hh, :])
        for hh in range(4):
            P.mm(pS_r[:, hh, :], kTr[:, hh // 2, :], qTr[:, hh, :])
        P.tt("dve", St_m, pS_m, Dt, ALU.mult)
        P.tt("dve", St_r, pS_r, decayT.re("p (h t) -> p h t", t=128), ALU.mult)
        for hh in range(4):
            P.mm(pS_m[:, hh, :], St_m[:, hh, :], vm[:, hh * 128:(hh + 1) * 128], start=True, stop=False)
            P.mm(pS_m[:, hh, :], qsTm[:, hh, :], stc(Cbf, hh), start=False, stop=True)
        for hh in range(4):
            P.mm(p_den[:, hh:hh + 1], St_m[:, hh, :], onesb[:, 0:1], start=True, stop=False)
            P.mm(p_den[:, hh:hh + 1], qsTm[:, hh, :], nbf[:, hh // 2:hh // 2 + 1], start=False, stop=True)
        for hh in range(4):
            P.mm(pS_r[:, hh, :], St_r[:, hh, :], vr[:, hh * 128:(hh + 1) * 128], start=True, stop=False)
            P.mm(pS_r[:, hh, :], qsTr[:, hh, :], stc(Rbf, hh), start=False, stop=True)
        P.ts("dve", d1, p_den, -1.0, ALU.mult)
        P.tt("dve", d1, d1, p_den, ALU.max)
        P.tt("dve", d2, d1, emt, ALU.max)
        P.recip(rec, d2)
        P.tt("dve", hm, pS_m, rec.re("p (h o) -> p h o", o=1).bc([128, 4, 128]), ALU.mult)
        P.act(gw_m, proj[:, 1032:1544], AF.Sigmoid)
        P.tt("pool", gw_m, gw_m, mnw, ALU.mult)
        _headnorm(P, hm, sq, nst, gw_m, cat[:, 0:512])
        P.cp("act", hr, pS_r)
        P.act(gw_r, proj[:, 2568:3080], AF.Silu)
        P.tt("pool", gw_r, gw_r, rnw, ALU.mult)
        _headnorm(P, hr, sq, nsr, gw_r, cat[:, 512:1024])
        for bb in range(2):
            P.mm(pdC[:, bb, :], kwm[:, bb * 128:(bb + 1) * 128], vm[:, bb * 256:(bb + 1) * 256])
        for bb in range(2):
            P.mm(p_dn[:, bb:bb + 1], kwm[:, bb * 128:(bb + 1) * 128], onesb[:, 0:1])
        for hh in range(4):
            so = sold[(hh % 2) * 64:(hh % 2) * 64 + 64, hh:hh + 1]
            P.stt("dve", stv(Cst, hh), stv(Cst, hh), so, stv(pdC, hh), ALU.mult, ALU.add)
            P.stt("dve", nv_(nstt, hh), nv_(nstt, hh), so, nv_(p_dn, hh), ALU.mult, ALU.add)
        P.cp("pool", Cbf, Cst)
        P.cp("pool", nbf, nstt)
        for bb in range(2):
            P.mm(pdR[:, bb, :], kwr[:, bb * 128:(bb + 1) * 128], vr[:, bb * 256:(bb + 1) * 256])
        for hh in range(4):
            cdv = cd[(hh % 2) * 64:(hh % 2) * 64 + 64, hh:hh + 1]
            P.stt("dve", stv(Rst, hh), stv(Rst, hh), cdv, stv(pdR, hh), ALU.mult, ALU.add)
        P.cp("pool", Rbf, Rst)
        for k in range(8):
            P.tr(pT[:, k * 128:(k + 1) * 128], cat[:, k * 128:(k + 1) * 128], identb)
        P.cp("act", catT, pT.re("p (a b) -> p a b", b=128))
        for n in range(2):
            for k in range(8):
                P.mm(pp[n], catT[:, k, :], w_out[:, k, n * 512:(n + 1) * 512], start=(k == 0), stop=(k == 7))
            P.tt("dve", h[:, n * 512:(n + 1) * 512], pp[n], h[:, n * 512:(n + 1) * 512], ALU.add)
        o = P.dma("sp", dst[r0:r0 + L, :], h, "io%d" % (c % 2))
        if out_stores is not None:
            out_stores.append(o)

    _front(0)
    for c in range(NCH):
        if c + 1 < NCH:
            _front(c + 1)
        _core(c)


def _headnorm(P, hv, sq, ns, gw, outv):
    s1 = ns[:, 0:4]
    s2 = ns[:, 4:8]
    mean = ns[:, 8:12]
    var = ns[:, 12:16]
    m2 = ns[:, 16:20]
    rstd = ns[:, 20:24]
    P.red("dve", s1, hv, ALU.add)
    P.tt("dve", sq, hv, hv, ALU.mult)
    P.red("dve", s2, sq, ALU.add)
    P.ts("dve", mean, s1, 1.0 / 128, ALU.mult)
    P.tt("dve", m2, mean, mean, ALU.mult)
    P.stt("dve", var, s2, 1.0 / 128, m2, ALU.mult, ALU.subtract)
    P.ts("dve", var, var, 1e-5, ALU.add)
    P.act(m2, var, AF.Ln)
    P.act(rstd, m2, AF.Exp, scale=-0.5)
    P.tt("dve", hv, hv, mean.re("p (h o) -> p h o", o=1).bc([128, 4, 128]), ALU.subtract)
    P.tt("dve", hv, hv, rstd.re("p (h o) -> p h o", o=1).bc([128, 4, 128]), ALU.mult)
    P.tt("dve", outv.re("p (h k) -> p h k", k=128), hv, gw.re("p (h k) -> p h k", k=128), ALU.mult)


def _phase_ssd(P, A, PS, nc, layer, src, dst, S, w_in_d, w_out_d, nmix_d, ssdp_d, small_load, cst_load,
               identb, identf, out_stores):
    j = layer // 2
    NCH = S // L
    w_in = A.alloc([128, 8, SSD_PROJ], BF16, "w_in")
    w_out = A.alloc([128, 16, D], BF16, "w_out")
    nwc = small_load([128, 8], nmix_d[layer], "nwc")
    small = small_load([128, 272], ssdp_d[j], "ssdsmall")
    cw = small[:, 0:128].re("p (i t) -> p i t", t=4)
    cb = small[:, 128:160]
    dtb = small[:, 160:192]
    alog = small[:, 192:224]
    dsk = small[:, 224:256]
    nsw = small[:, 256:272]
    _load_weight(P, w_in, w_in_d[j], 8, 1, "w1")
    _load_weight(P, w_out, w_out_d[j], 16, 4, "w2")
    for k in range(8):
        _fold(P, k, w_in[:, k, :], nwc[:, k:k + 1])
    for k in range(16):
        _fold(P, k, w_out[:, k, :], nsw[:, k:k + 1])
    triPos = cst_load("triPos")
    triPosb = A.alloc([128, 128], BF16, "triPosb")
    P.cp("pool", triPosb, triPos)
    mask01B = cst_load("mask01B")
    e127 = cst_load("e127")
    Aneg = A.alloc([128, 32], F32, "Aneg")
    P.act(Aneg, alog, AF.Exp)
    P.ts("dve", Aneg, Aneg, -1.0, ALU.mult)

    hc = [A.alloc([128, D], F32, "hc%d" % i) for i in range(2)]
    u = A.alloc([128, D], BF16, "u")
    uT = A.alloc([128, 8, 128], BF16, "uT")
    stt = A.alloc([128, 8], F32, "stt")
    xpre = [A.alloc([128, 4, 131], BF16, "xpre%d" % q) for q in range(8)]
    xpre_all = V(None, [x_.bufs[0] for x_ in xpre])
    accw, acc = [], []
    for s_ in range(2):
        t_ = A.alloc([128, 4, 128], F32, "acc%d" % s_)
        bl = [Buf("acc%d_%d" % (s_, ii)) for ii in range(4)]
        accw.append(V(t_.ap, bl))
        acc.append([V(t_.ap[:, ii, :], [bl[ii]]) for ii in range(4)])
    ptmp = [A.alloc([128, 128], F32, "ptmp%d" % i) for i in range(4)]
    xTt = A.alloc([128, 4, 128], BF16, "xTt")
    BT = A.alloc([128, 8, 128], BF16, "BT")
    CT = A.alloc([128, 8, 128], BF16, "CT")
    x_tok = A.alloc([128, 2048], BF16, "x_tok")
    xdt = A.alloc([128, 2048], BF16, "xdt")
    B_tok = A.alloc([128, 1024], BF16, "B_tok")
    zs = A.alloc([128, 2048], BF16, "zs")
    Ee = A.alloc([128, 4, 128], F32, "Ee")
    E2 = A.alloc([128, 4, 128], F32, "E2")
    CBm = A.alloc([128, 128], F32, "CBm")
    MT = A.alloc([128, 4, 128], BF16, "MT")
    CTs = A.alloc([128, 4, 128], BF16, "CTs")
    xw = A.alloc([128, 256], BF16, "xw")
    ytmp = A.alloc([128, 256], F32, "ytmp")
    Ee_l = [Ee, accw[0]]
    E2_l = [E2, accw[1]]
    MT_l = [MT, xTt]
    CTs_l = [CTs, A.alloc([128, 4, 128], BF16, "CTs2")]
    CBm_l = [CBm, A.alloc([128, 128], F32, "CBm2")]
    ytmp_l = [ytmp, A.alloc([128, 256], F32, "ytmp2")]
    xw_l = [xw, A.alloc([128, 256], BF16, "xw2")]
    ygpre = A.alloc([128, 2048], BF16, "ygpre")
    ygT = V(xdt.ap.rearrange("p (a b) -> p a b", b=128), xdt.bufs)
    junk = V(xdt.ap[:, 0:1024], xdt.bufs)
    Hs = [A.alloc([128, 256], F32, "H%d" % g) for g in range(8)]
    Hb = [A.alloc([128, 256], BF16, "Hb%d" % g) for g in range(8)]

    def sm(n, name):
        return A.alloc([128, n], F32, name)
    dtp = sm(32, "dtp")
    edt = sm(32, "edt")
    dt = sm(32, "dt")
    dtA = sm(32, "dtA")
    dhi = A.alloc([128, 32], BF16, "dhi")
    dlo = A.alloc([128, 32], BF16, "dlo")
    cs = sm(32, "cs")
    tea = sm(32, "tea")
    te = sm(32, "te")
    dec = sm(32, "dec")
    ssq8 = sm(8, "ssq8")
    ms8 = sm(8, "ms8")
    ln8 = sm(8, "ln8")
    rstd8 = sm(8, "rstd8")

    pp = [PS.view(b_, 0, 512) for b_ in (0, 1)]
    pf = [PS.view(b_, 0, 512, shape=[4, 128]) for b_ in (2, 3)]
    pT = PS.view(4, 0, 512, BF16)
    pcsb_l = [PS.view(5, 0, 512, shape=[4, 128]), PS.view(2, 0, 512, shape=[4, 128])]
    pCB = PS.view(6, 0, 128)
    p_cs = PS.view(6, 128, 32)
    p_csl = PS.view(6, 160, 32)
    py_l = [PS.view(7, 0, 256), PS.view(3, 0, 256)]
    pdH_l = [PS.view(0, 0, 256), PS.view(1, 0, 256)]

    for g in range(8):
        P.memset("pool", Hs[g], 0.0)
        P.memset("pool", Hb[g], 0.0)
    for q in range(8):
        P.memset("pool", xpre[q], 0.0)

    def _head(c):
        h = hc[c % 2]
        r0 = c * L
        P.dma("sp", h, src[r0:r0 + L, :], "io%d" % (c % 2))
        P.act(junk, h, AF.Square, accum=stt[:, 0:1])
        _rms_scale(P, stt[:, 0:1], stt[:, 1:2], stt[:, 2:3], stt[:, 3:4], D, 1e-6)
        P.ts("dve", u, h, stt[:, 3:4], ALU.mult)
        for k in range(8):
            P.tr(pT[:, k * 128:(k + 1) * 128], u[:, k * 128:(k + 1) * 128], identb)
        P.cp("act", uT, pT.re("p (a b) -> p a b", b=128))
        for bi in range(4):
            pb = pp[bi % 2]
            for k in range(8):
                P.mm(pb, uT[:, k, :], w_in[:, k, bi * 512:(bi + 1) * 512], start=(k == 0), stop=(k == 7))
            P.act(zs[:, bi * 512:(bi + 1) * 512], pb, AF.Silu)
        pb = pp[0]
        for k in range(8):
            P.mm(pb[:, 0:32], uT[:, k, :], w_in[:, k, 6144:6176], start=(k == 0), stop=(k == 7))
        P.tt("dve", dtp, pb[:, 0:32], dtb, ALU.add)
        P.act(edt, dtp, AF.Exp)
        P.act(dt, edt, AF.Ln, bias=1.0)
        P.tt("dve", dtA, dt, Aneg, ALU.mult)
        P.mm(p_cs, triPos, dtA)
        P.cp("dve", cs, p_cs)
        P.mm(p_csl, e127, cs)
        P.tt("dve", tea, p_csl, cs, ALU.subtract)
        P.act(te, tea, AF.Exp)
        P.act(dec, p_csl, AF.Exp)
        if c > 0:
            for q in range(8):
                P.cp("pool", xpre[q][:, :, 0:3], xpre[q][:, :, 128:131])
        for q in range(8):
            pfb = pf[q % 2]
            for ii in range(4):
                i = q * 4 + ii
                for k in range(8):
                    P.mm(pfb[:, ii, :], w_in[:, k, 2048 + i * 128:2048 + (i + 1) * 128], uT[:, k, :],
                         start=(k == 0), stop=(k == 7))
            P.cp("act" if q % 2 else "dve", xpre[q][:, :, 3:131], pfb)
    def _mid(c):
        h = hc[c % 2]
        r0 = c * L
        for q in range(8):
            ac = acc[q % 2]
            xis = [xpre[q][:, ii, :] for ii in range(4)]
            for ii in range(4):
                i = q * 4 + ii
                P.act(ac[ii], xis[ii][:, 3:131], AF.Identity, bias=cb[:, i:i + 1], scale=cw[:, i, 3:4])
            for ii in range(4):
                i = q * 4 + ii
                P.act(ptmp[ii], xis[ii][:, 0:128], AF.Identity, scale=cw[:, i, 0:1])
            for ii in range(4):
                i = q * 4 + ii
                P.stt("dve", ac[ii], xis[ii][:, 2:130], cw[:, i, 2:3], ac[ii], ALU.mult, ALU.add)
            for ii in range(4):
                i = q * 4 + ii
                P.stt("dve", ac[ii], xis[ii][:, 1:129], cw[:, i, 1:2], ac[ii], ALU.mult, ALU.add)
            for ii in range(4):
                P.tt("pool", ac[ii], ac[ii], ptmp[ii], ALU.add)
            if q < 4:
                dsts = [xTt[:, ii, :] for ii in range(4)]
            elif q < 6:
                dsts = [BT[:, (q - 4) * 4 + ii, :] for ii in range(4)]
            else:
                dsts = [CT[:, (q - 6) * 4 + ii, :] for ii in range(4)]
            if q < 4:
                dwhole = xTt
            elif q < 6:
                dwhole = BT[:, (q - 4) * 4:(q - 4) * 4 + 4, :]
            else:
                dwhole = CT[:, (q - 6) * 4:(q - 6) * 4 + 4, :]
            P.act(dwhole, accw[q % 2], AF.Silu)
            if q < 6:
                for ii in range(4):
                    P.tr(pT[:, ii * 128:(ii + 1) * 128], dsts[ii], identb)
                if q < 4:
                    P.cp("dve", x_tok[:, q * 512:(q + 1) * 512], pT[:, 0:512])
                else:
                    P.cp("dve", B_tok[:, (q - 4) * 512:(q - 3) * 512], pT[:, 0:512])
        x3 = x_tok.re("p (h k) -> p h k", k=64)
        P.tt("dve", xdt.re("p (h k) -> p h k", k=64), x3, dt.re("p (h o) -> p h o", o=1).bc([128, 32, 64]), ALU.mult)
        P.tt("pool", x3, x3, dsk.re("p (h o) -> p h o", o=1).bc([128, 32, 64]), ALU.mult)
        def _grp_a(g):
            pcsb = pcsb_l[g % 2]
            py = py_l[g % 2]
            pdH = pdH_l[g % 2]
            Ee = Ee_l[g % 2]
            E2 = E2_l[g % 2]
            MT = MT_l[g % 2]
            CTs = CTs_l[g % 2]
            CBm = CBm_l[g % 2]
            ytmp = ytmp_l[g % 2]
            xw = xw_l[g % 2]
            for jj in range(4):
                hd = 4 * g + jj
                P.mm(pcsb[:, jj, :], dtA[:, hd:hd + 1].bc([128, 128]), triPos)
            P.mm(pCB, BT[:, g, :], CT[:, g, :])
            if SSD_SUB <= 'a':
                return
            P.tt("dve", CBm, pCB, mask01B, ALU.mult)
            for jj in range(4):
                P.ts("dve", Ee[:, jj, :], pcsb[:, jj, :], cs[:, 4 * g + jj:4 * g + jj + 1], ALU.subtract, 0.0, ALU.min)
            P.act(E2, pcsb, AF.Exp)
            P.act(Ee, Ee, AF.Exp)
        def _grp_b(g):
            pcsb = pcsb_l[g % 2]
            py = py_l[g % 2]
            pdH = pdH_l[g % 2]
            Ee = Ee_l[g % 2]
            E2 = E2_l[g % 2]
            MT = MT_l[g % 2]
            CTs = CTs_l[g % 2]
            CBm = CBm_l[g % 2]
            ytmp = ytmp_l[g % 2]
            xw = xw_l[g % 2]
            P.tt("dve", MT, Ee, CBm.re("p (o t) -> p o t", o=1).bc([128, 4, 128]), ALU.mult)
            P.tt("dve", CTs, E2, CT[:, g, :].re("p (o t) -> p o t", o=1).bc([128, 4, 128]), ALU.mult)
            for jj in range(4):
                hd = 4 * g + jj
                P.mm(py[:, jj * 64:(jj + 1) * 64], MT[:, jj, :], xdt[:, hd * 64:(hd + 1) * 64], start=True, stop=False)
                P.mm(py[:, jj * 64:(jj + 1) * 64], CTs[:, jj, :], Hb[g][:, jj * 64:(jj + 1) * 64], start=False, stop=True)
            if SSD_SUB <= 'c':
                return
            gs = slice(g * 256, (g + 1) * 256)
            P.tt("dve", ytmp, py, x_tok[:, gs], ALU.add)
            P.tt("pool", ygpre[:, gs], zs[:, gs], ytmp, ALU.mult)
            P.act(u[:, 0:256], ygpre[:, gs], AF.Square, accum=ssq8[:, g:g + 1])
            if SSD_SUB <= 'd':
                return
            P.tt("pool", xw.re("p (r k) -> p r k", k=64), xdt[:, gs].re("p (r k) -> p r k", k=64),
                 te[:, 4 * g:4 * g + 4].re("p (r o) -> p r o", o=1).bc([128, 4, 64]), ALU.mult)
            P.mm(pdH, B_tok[:, g * 128:(g + 1) * 128], xw)
            P.tt("pool", Hs[g].re("p (r k) -> p r k", k=64), Hs[g].re("p (r k) -> p r k", k=64),
                 dec[:, 4 * g:4 * g + 4].re("p (r o) -> p r o", o=1).bc([128, 4, 64]), ALU.mult)
            P.tt("dve", Hs[g], Hs[g], pdH, ALU.add)
            P.cp("pool", Hb[g], Hs[g])
        _grp_a(0)
        for g in range(8):
            if g + 1 < 8:
                _grp_a(g + 1)
            _grp_b(g)
    def _tail(c):
        h = hc[c % 2]
        r0 = c * L
        _rms_scale(P, ssq8, ms8, ln8, rstd8, 256, 1e-5)
        yg3 = ygpre.re("p (g k) -> p g k", k=256)
        P.tt("dve", yg3, yg3, rstd8.re("p (g o) -> p g o", o=1).bc([128, 8, 256]), ALU.mult)
        for rnd in range(2):
            for k in range(8):
                kk = rnd * 8 + k
                P.tr(pT[:, k * 128:(k + 1) * 128], ygpre[:, kk * 128:(kk + 1) * 128], identb)
            P.cp("act", ygT[:, rnd * 8:(rnd + 1) * 8, :], pT.re("p (a b) -> p a b", b=128))
        for n in range(2):
            for k in range(16):
                P.mm(pp[n], ygT[:, k, :], w_out[:, k, n * 512:(n + 1) * 512], start=(k == 0), stop=(k == 15))
            P.tt("dve", h[:, n * 512:(n + 1) * 512], pp[n], h[:, n * 512:(n + 1) * 512], ALU.add)
        o = P.dma("sp", dst[r0:r0 + L, :], h, "io%d" % (c % 2))
        if out_stores is not None:
            out_stores.append(o)

    _head(0)
    for c in range(NCH):
        _mid(c)
        if c + 1 < NCH:
            _head(c + 1)
        _tail(c)


FULL_PLAN = [("hyb", 0), ("mlp", 0), ("ssd", 1), ("mlp", 1), ("hyb", 2), ("mlp", 2), ("ssd", 3), ("mlp", 3)]


def _col(v, k):
    return np.ascontiguousarray(np.asarray(v, np.float32).reshape(k, 128).T)


def _bc(v):
    v = np.asarray(v, np.float32).reshape(1, -1)
    return np.ascontiguousarray(np.broadcast_to(v, (128, v.shape[1])))


def make_in_maps(inputs, S, n_cores):
    cst_np, _ = _build_consts()
    f = lambda k: np.ascontiguousarray(np.asarray(inputs[k], np.float32))
    nmix = np.stack([_col(inputs["norm_mix_w"][l], 8) for l in range(4)])
    nmlp = np.stack([_col(inputs["norm_mlp_w"][l], 8) for l in range(4)])
    hyb_small = np.stack([np.concatenate([_bc(inputs["mlstm_i_bias"][j]), _bc(inputs["mlstm_f_bias"][j]),
                                          _bc(inputs["mlstm_norm_w"][j]), _bc(inputs["ret_norm_w"][j])], axis=1)
                          for j in range(2)])
    ssd_small = []
    for j in range(2):
        cw = np.asarray(inputs["ssd_conv_w"][j], np.float32)
        cwl = cw.reshape(4, 32, 128).transpose(2, 1, 0).reshape(128, 128)
        cb = _col(inputs["ssd_conv_b"][j], 32)
        ssd_small.append(np.concatenate([cwl, cb, _bc(inputs["ssd_dt_bias"][j]), _bc(inputs["ssd_a_log"][j]),
                                         _bc(inputs["ssd_d"][j]), _col(inputs["ssd_norm_w"][j], 16)], axis=1))
    ssd_small = np.ascontiguousarray(np.stack(ssd_small).astype(np.float32))
    shared = {
        "hyb_w_in": f("hyb_w_in"), "hyb_w_out": f("hyb_w_out"), "ssd_w_in": f("ssd_w_in"), "ssd_w_out": f("ssd_w_out"),
        "mlp_w1": f("mlp_w1"), "mlp_w2": f("mlp_w2"), "cst": cst_np, "rope": _rope_tables(S),
        "nmix_col": nmix, "nmlp_col": nmlp, "fnw_b": _bc(inputs["final_norm_w"]),
        "hyb_small": np.ascontiguousarray(hyb_small.astype(np.float32)), "ssd_small": ssd_small,
    }
    x = np.asarray(inputs["x"], np.float32)
    maps = []
    for i in range(n_cores):
        m = dict(shared)
        m["x"] = np.ascontiguousarray(x[i, :S])
        maps.append(m)
    return maps


_NC_CACHE = {}


def kernel(**inputs):
    S = inputs["x"].shape[1]
    B = inputs["x"].shape[0]
    key = (S, "full")
    if key not in _NC_CACHE:
        _NC_CACHE[key] = build_program(S, FULL_PLAN, final_norm=True)
    nc = _NC_CACHE[key]
    maps = make_in_maps(inputs, S, B)
    res = run_bass_kernel_spmd(nc, maps, core_ids=list(range(B)))
    return np.stack([np.asarray(r["out"], np.float32) for r in res.results], axis=0)
```

```python
import contextlib
import math
import numpy as np
import concourse.bass as bass
import concourse.mybir as mybir
from concourse.bass_utils import run_bass_kernel_spmd

F32 = mybir.dt.float32
BF16 = mybir.dt.bfloat16
ALU = mybir.AluOpType
AF = mybir.ActivationFunctionType
AX = mybir.AxisListType

D = 1024
L = 128
DFF = 4096
HYB_PROJ = 3080
SSD_PROJ = 6176
NEG = -30000.0
import os
HYB_STOP = int(os.environ.get('HYB_STOP', '99'))
SSD_STOP = int(os.environ.get('SSD_STOP', '99'))
SSD_SUB = os.environ.get('SSD_SUB', 'z')
ENGS = ("pe", "act", "dve", "pool", "sp")


class Buf:
    __slots__ = ("name", "writer", "readers", "excl")

    def __init__(self, name="", excl=False):
        self.name = name
        self.writer = None
        self.readers = []
        self.excl = excl


class V:
    __slots__ = ("ap", "bufs")

    def __init__(self, ap, bufs):
        self.ap = ap
        self.bufs = tuple(bufs)

    def __getitem__(self, idx):
        return V(self.ap[idx], self.bufs)

    def bc(self, shape):
        return V(self.ap.broadcast_to(list(shape)), self.bufs)

    def re(self, s, **kw):
        return V(self.ap.rearrange(s, **kw), self.bufs)


def _ap(x):
    return x.ap if isinstance(x, V) else x


def _bufs(*xs):
    out = []
    for x in xs:
        if isinstance(x, V):
            out.extend(x.bufs)
    return out


class Op:
    __slots__ = ("eng", "fn", "deps", "is_dma", "sem", "sigval", "needs_sig", "idx")

    def __init__(self, eng, fn, is_dma, sem):
        self.eng = eng
        self.fn = fn
        self.deps = []
        self.is_dma = is_dma
        self.sem = sem
        self.sigval = None
        self.needs_sig = False


class Prog:
    def __init__(self, nc):
        self.nc = nc
        self.ops = []
        self.phase = 0
        self.last_on = {}
        self.dmas_since = []
        self.gate = None
        self.gate_passed = set()

    def barrier(self):
        deps = list(self.last_on.values()) + list(self.dmas_since)
        self.gate = deps
        self.gate_passed = set()
        self.dmas_since = []
        self.phase += 1

    def op(self, eng, fn, reads=(), writes=(), dma_sem=None, skip_waw=False):
        is_dma = dma_sem is not None
        sem = ("dma", dma_sem, self.phase) if is_dma else ("eng", eng, self.phase)
        o = Op(eng, fn, is_dma, sem)
        o.idx = len(self.ops)
        deps = []
        for b in reads:
            if b.writer is not None:
                deps.append(b.writer)
            if b.excl:
                deps.extend(r for r in b.readers if r.eng != eng)
        for b in writes:
            if b.writer is not None and not skip_waw:
                deps.append(b.writer)
            deps.extend(b.readers)
        if self.gate is not None and eng not in self.gate_passed:
            deps.extend(self.gate)
            self.gate_passed.add(eng)
        latest = {}
        seen = set()
        for d in deps:
            if d is o or id(d) in seen:
                continue
            seen.add(id(d))
            if d.eng == "pe" and eng == "pe" and not d.is_dma and not is_dma:
                continue
            if d.is_dma:
                o.deps.append(d)
                d.needs_sig = True
            else:
                cur = latest.get(d.sem)
                if cur is None or d.idx > cur.idx:
                    latest[d.sem] = d
        for d in latest.values():
            o.deps.append(d)
            d.needs_sig = True
        for b in reads:
            b.readers.append(o)
        for b in writes:
            b.writer = o
            b.readers = []
        self.ops.append(o)
        self.last_on[eng] = o
        if is_dma:
            self.dmas_since.append(o)
            o.needs_sig = True
        return o

    def mm(self, out, lhsT, rhs, start=True, stop=True):
        return self.op("pe", lambda e: e.matmul(_ap(out), lhsT=_ap(lhsT), rhs=_ap(rhs), start=start, stop=stop),
                       reads=_bufs(lhsT, rhs), writes=_bufs(out))

    def tr(self, out, in_, ident):
        return self.op("pe", lambda e: e.transpose(_ap(out), _ap(in_), _ap(ident)),
                       reads=_bufs(in_, ident), writes=_bufs(out))

    def act(self, out, in_, func, bias=None, scale=None, accum=None):
        kw = {}
        if bias is not None:
            kw["bias"] = _ap(bias)
        if scale is not None:
            kw["scale"] = _ap(scale)
        if accum is not None:
            kw["accum_out"] = _ap(accum)
        return self.op("act", lambda e: e.activation(out=_ap(out), in_=_ap(in_), func=func, **kw),
                       reads=_bufs(in_, bias, scale), writes=_bufs(out, accum))

    def tt(self, eng, out, in0, in1, op):
        return self.op(eng, lambda e: e.tensor_tensor(out=_ap(out), in0=_ap(in0), in1=_ap(in1), op=op),
                       reads=_bufs(in0, in1), writes=_bufs(out))

    def ts(self, eng, out, in0, s1, op0, s2=None, op1=None):
        if op1 is None:
            return self.op(eng, lambda e: e.tensor_scalar(out=_ap(out), in0=_ap(in0), scalar1=_ap(s1), scalar2=None, op0=op0),
                           reads=_bufs(in0, s1), writes=_bufs(out))
        return self.op(eng, lambda e: e.tensor_scalar(out=_ap(out), in0=_ap(in0), scalar1=_ap(s1), scalar2=_ap(s2),
                                                      op0=op0, op1=op1),
                       reads=_bufs(in0, s1, s2), writes=_bufs(out))

    def stt(self, eng, out, in0, scalar, in1, op0, op1):
        return self.op(eng, lambda e: e.scalar_tensor_tensor(out=_ap(out), in0=_ap(in0), scalar=_ap(scalar), in1=_ap(in1),
                                                             op0=op0, op1=op1),
                       reads=_bufs(in0, scalar, in1), writes=_bufs(out))

    def cp(self, eng, out, in_):
        if eng == "act":
            return self.op("act", lambda e: e.copy(out=_ap(out), in_=_ap(in_)), reads=_bufs(in_), writes=_bufs(out))
        return self.op(eng, lambda e: e.tensor_copy(out=_ap(out), in_=_ap(in_)), reads=_bufs(in_), writes=_bufs(out))

    def red(self, eng, out, in_, op):
        return self.op(eng, lambda e: e.tensor_reduce(out=_ap(out), in_=_ap(in_), axis=AX.X, op=op),
                       reads=_bufs(in_), writes=_bufs(out))

    def memset(self, eng, out, val):
        return self.op(eng, lambda e: e.memset(_ap(out), val), writes=_bufs(out))

    def recip(self, out, in_):
        return self.op("dve", lambda e: e.reciprocal(out=_ap(out), in_=_ap(in_)), reads=_bufs(in_), writes=_bufs(out))

    def dma(self, q, out, in_, sem, skip_waw=False):
        return self.op(q, lambda e: e.dma_start(out=_ap(out), in_=_ap(in_)), reads=_bufs(in_), writes=_bufs(out),
                       dma_sem=sem, skip_waw=skip_waw)

    def emit(self, final_wait_ops=()):
        nc = self.nc
        counters = {}
        for o in final_wait_ops:
            o.needs_sig = True
        for o in self.ops:
            if o.needs_sig:
                inc = 16 if o.is_dma else 1
                counters[o.sem] = counters.get(o.sem, 0) + inc
                o.sigval = counters[o.sem]
        self.max_sem = counters
        with contextlib.ExitStack() as st:
            sems = {}
            for i, k in enumerate(counters.keys()):
                sems[k] = st.enter_context(nc.semaphore("s%d" % i))
            block = st.enter_context(nc.Block())
            per_eng = {e: [o for o in self.ops if o.eng == e] for e in ENGS}

            def make_body(e):
                def body(eng):
                    waited = {}
                    for o in per_eng[e]:
                        need = {}
                        for d in o.deps:
                            if need.get(d.sem, 0) < d.sigval:
                                need[d.sem] = d.sigval
                        for k, v in need.items():
                            if waited.get(k, 0) >= v:
                                continue
                            eng.wait_ge(sems[k], v)
                            waited[k] = v
                        ins = o.fn(eng)
                        if o.needs_sig:
                            ins.then_inc(sems[o.sem], 16 if o.is_dma else 1)
                    if e == "sp":
                        fin = {}
                        for o in final_wait_ops:
                            fin[o.sem] = max(fin.get(o.sem, 0), o.sigval)
                        for k, v in fin.items():
                            eng.wait_ge(sems[k], v)
                return body

            regs = {"pe": block.tensor, "act": block.scalar, "dve": block.vector,
                    "pool": block.gpsimd, "sp": block.sync}
            for e in ENGS:
                regs[e](make_body(e))


class Arena:
    def __init__(self, tensor, nbytes):
        self.t = tensor
        self.nbytes = nbytes
        self.off = 0
        self.base = 0

    def reset(self):
        self.off = self.base

    def alloc(self, shape, dt, name="", nbufs=None):
        n = 1
        for s in shape[1:]:
            n *= s
        esz = 2 if dt == BF16 else 4
        nb = (n * esz + 31) // 32 * 32
        assert self.off + nb <= self.nbytes, "arena overflow %s: %d + %d > %d" % (name, self.off, nb, self.nbytes)
        o4 = self.off // 4
        ap = self.t[:, o4:o4 + nb // 4]
        if dt == BF16:
            ap = ap.bitcast(BF16)
        ap = ap[0:shape[0], 0:n]
        if len(shape) == 3:
            ap = ap.rearrange("p (a b) -> p a b", b=shape[2])
        elif len(shape) == 4:
            ap = ap.rearrange("p (a b c) -> p a b c", b=shape[2], c=shape[3])
        self.off += nb
        return V(ap, [Buf(name)])


class Psum:
    def __init__(self, tensor):
        self.t = tensor
        self.bufs = {}

    def new_phase(self):
        self.bufs = {}

    def view(self, bank, col0, ncols, dt=F32, shape=None, sub=0, parts=128):
        key = (bank, 0)
        if key not in self.bufs:
            self.bufs[key] = Buf("ps%d_%s" % (bank, sub), excl=True)
        ap = self.t[:, bank * 512 + col0: bank * 512 + col0 + ncols]
        if dt == BF16:
            ap = ap.bitcast(BF16)
        ap = ap[0:parts]
        if shape is not None:
            if len(shape) == 2:
                ap = ap.rearrange("p (a b) -> p a b", b=shape[1])
        return V(ap, [self.bufs[key]])


CST = {}


def _build_consts():
    c = {}
    idx = np.arange(128)
    p = idx[:, None]
    j = idx[None, :]
    c["identf"] = (p == j).astype(np.float32)
    c["triNeg"] = -(p <= j).astype(np.float32)
    c["triPos"] = (p <= j).astype(np.float32)
    c["maskA"] = np.where(j <= p, 0.0, NEG).astype(np.float32)
    c["maskB"] = np.where(j >= p, 0.0, NEG).astype(np.float32)
    c["mask01B"] = (j >= p).astype(np.float32)
    c["e127"] = np.zeros((128, 128), np.float32)
    c["e127"][127, :] = 1.0
    lg = np.log(1.0 - 2.0 ** (-5.0 - np.arange(4, dtype=np.float64)))
    rel = (j - p).astype(np.float64)
    dec = np.where((j >= p)[:, None, :], np.exp(np.maximum(rel, 0.0)[:, None, :] * lg[None, :, None]), 0.0)
    c["decayT"] = dec.reshape(128, 512).astype(np.float32)
    c["wq8"] = (np.exp((idx[:, None] + 1.0) * lg[None, :]) * 0.125).astype(np.float32)
    c["wk"] = np.exp((127.0 - idx[:, None]) * lg[None, :]).astype(np.float32)
    c["cd"] = np.broadcast_to(np.exp(128.0 * lg)[None, :], (128, 4)).astype(np.float32)
    c["ones"] = np.ones((128, 8), np.float32)
    order = ["identf", "triNeg", "triPos", "maskA", "maskB", "mask01B", "e127", "decayT", "wq8", "wk", "cd", "ones"]
    offs = {}
    o = 0
    for k in order:
        offs[k] = (o, c[k].shape[1])
        o += c[k].shape[1]
    return np.concatenate([c[k] for k in order], axis=1), offs


def _rope_tables(S):
    inv = 10000.0 ** (-np.arange(0, 64, 2, dtype=np.float32) / np.float32(64))
    ang = np.arange(S, dtype=np.float32)[:, None] * inv[None, :].astype(np.float32)
    ang = ang.astype(np.float32).astype(np.float64)
    cos = np.cos(ang).astype(np.float32).reshape(S // 128, 128, 32).transpose(1, 0, 2)
    sin = np.sin(ang).astype(np.float32).reshape(S // 128, 128, 32).transpose(1, 0, 2)
    return np.ascontiguousarray(np.concatenate([cos.reshape(128, -1), sin.reshape(128, -1)], axis=1))


def build_program(S, plan, final_norm=True):
    NCH = S // L
    nc = bass.Bass("TRN2", target_bir_lowering=False)
    cst_np, coffs = _build_consts()
    NCST = cst_np.shape[1]

    def din(name, shape):
        return nc.dram_tensor(name, list(shape), F32, kind="ExternalInput").ap()

    x_d = din("x", [S, D])
    hyb_w_in = din("hyb_w_in", [2, D, HYB_PROJ])
    hyb_w_out = din("hyb_w_out", [2, D, D])
    ssd_w_in = din("ssd_w_in", [2, D, SSD_PROJ])
    ssd_w_out = din("ssd_w_out", [2, 2048, D])
    mlp_w1 = din("mlp_w1", [4, D, DFF])
    mlp_w2 = din("mlp_w2", [4, DFF, D])
    cst_d = din("cst", [128, NCST])
    rope_d = din("rope", [128, 2 * NCH * 32])
    nmix_d = din("nmix_col", [4, 128, 8])
    nmlp_d = din("nmlp_col", [4, 128, 8])
    fnw_d = din("fnw_b", [128, D])
    hybp_d = din("hyb_small", [2, 128, 8 + 1024])
    ssdp_d = din("ssd_small", [2, 128, 128 + 32 + 96 + 16])
    out_d = nc.dram_tensor("out", [S, D], F32, kind="ExternalOutput").ap()
    hbuf = nc.dram_tensor("hbuf", [S, D], F32).ap()

    ARENA_BYTES = 211968
    with contextlib.ExitStack() as st:
        arena_t = st.enter_context(nc.sbuf_tensor("arena", [128, ARENA_BYTES // 4], F32))
        psum_t = st.enter_context(nc.psum_tensor("psum", [128, 4096], F32))
        P = Prog(nc)
        A = Arena(arena_t, ARENA_BYTES)
        PS = Psum(psum_t)

        identf = A.alloc([128, 128], F32, "identf")
        identb = A.alloc([128, 128], BF16, "identb")
        P.dma("act", identf, cst_d[:, coffs["identf"][0]:coffs["identf"][0] + 128], "cid")
        P.cp("dve", identb, identf)
        A.base = A.off

        cbuf = [None]

        def small_load(shape, src_ap, name):
            t = A.alloc(shape, F32, name)
            t = V(t.ap, [cbuf[0]])
            P.dma("act", t, src_ap, "c", skip_waw=True)
            return t

        def cst_load(name, dt=F32):
            o, n = coffs[name]
            t = small_load([128, n], cst_d[:, o:o + n], name)
            if dt == BF16:
                tb = A.alloc([128, n], BF16, name + "b")
                P.cp("pool", tb, t)
                return tb
            return t

        out_stores = []
        src = x_d
        for pi, (kind, layer) in enumerate(plan):
            last = pi == len(plan) - 1
            dst = out_d if last else hbuf
            A.reset()
            PS.new_phase()
            cbuf[0] = Buf("consts%d" % pi)
            if kind == "mlp":
                _phase_mlp(P, A, PS, nc, layer, src, dst, S, mlp_w1, mlp_w2, nmlp_d, fnw_d, small_load, identb,
                           final_norm and last, out_stores if last else None)
            elif kind == "hyb":
                _phase_hyb(P, A, PS, nc, layer, src, dst, S, hyb_w_in, hyb_w_out, nmix_d, hybp_d, rope_d,
                           small_load, cst_load, identb, identf, out_stores if last else None)
            else:
                _phase_ssd(P, A, PS, nc, layer, src, dst, S, ssd_w_in, ssd_w_out, nmix_d, ssdp_d,
                           small_load, cst_load, identb, identf, out_stores if last else None)
            P.barrier()
            src = hbuf
        P.emit(final_wait_ops=out_stores)
    return nc


def _rms_scale(P, ssq, ms, ln, rstd, n, eps):
    P.ts("dve", ms, ssq, 1.0 / n, ALU.mult, eps, ALU.add)
    P.act(ln, ms, AF.Ln)
    P.act(rstd, ln, AF.Exp, scale=-0.5)


def _fold(P, k, wv, col):
    if k % 2:
        P.act(wv, wv, AF.Identity, scale=col)
    else:
        P.ts("dve", wv, wv, col, ALU.mult)


def _load_weight(P, wbf, wd, ktiles, per, sem):
    wv = wd.rearrange("(k p) n -> p k n", p=128)
    for k0 in range(0, ktiles, per):
        P.dma("pool", wbf[:, k0:k0 + per, :], wv[:, k0:k0 + per, :], sem, skip_waw=True)


def _phase_mlp(P, A, PS, nc, layer, src, dst, S, w1_d, w2_d, nmlp_d, fnw_d, small_load, identb, final, out_stores):
    TT = 512
    NT = S // TT
    w1 = A.alloc([128, 8, DFF], BF16, "w1")
    w2 = A.alloc([128, 32, D], BF16, "w2")
    nwc = small_load([128, 8], nmlp_d[layer], "nwc")
    _load_weight(P, w1, w1_d[layer], 8, 1, "w1")
    _load_weight(P, w2, w2_d[layer], 32, 4, "w2")
    for k in range(8):
        _fold(P, k, w1[:, k, :], nwc[:, k:k + 1])
    fnw = None
    if final:
        fnw = small_load([128, D], fnw_d, "fnw")
    ht = [A.alloc([128, D], F32, "ht%d" % j) for j in range(4)]
    u = [A.alloc([128, D], BF16, "u%d" % i) for i in range(2)]
    uTj = [None] * 4
    uT_all = A.alloc([128, 8, TT], BF16, "uT")
    ubufs = [Buf("uT%d" % j) for j in range(4)]
    for j in range(4):
        uTj[j] = V(uT_all.ap[:, :, j * 128:(j + 1) * 128], [ubufs[j]])
    uT = V(uT_all.ap, ubufs)
    h1T = A.alloc([128, 32, TT], BF16, "h1T")
    tmp = [A.alloc([128, TT], F32, "tmp%d" % i) for i in range(2)]
    junk = A.alloc([128, D], BF16, "junk")
    stt = [A.alloc([128, 8], F32, "io%d" % j) for j in range(4)]
    pT = [PS.view(b, 0, 512, BF16) for b in (0, 1)]
    pm1 = [PS.view(b, 0, 512) for b in (2, 3)]
    pm2 = [PS.view(b, 0, 512) for b in (4, 5, 6, 7)]

    for T in range(NT):
        for j in range(4):
            r0 = T * TT + j * 128
            P.dma("sp", ht[j], src[r0:r0 + 128, :], "io%d" % j)
            s = stt[j]
            P.act(junk, ht[j], AF.Square, accum=s[:, 0:1])
            _rms_scale(P, s[:, 0:1], s[:, 1:2], s[:, 2:3], s[:, 3:4], D, 1e-6)
            uu = u[j % 2]
            P.ts("dve", uu, ht[j], s[:, 3:4], ALU.mult)
            for k in range(8):
                P.tr(pT[j % 2][:, k * 128:(k + 1) * 128], uu[:, k * 128:(k + 1) * 128], identb)
            P.cp("dve" if j % 2 else "act", uTj[j], pT[j % 2].re("p (a b) -> p a b", b=128))
        for f in range(32):
            pb = pm1[f % 2]
            for k in range(8):
                P.mm(pb, w1[:, k, f * 128:(f + 1) * 128], uT[:, k, :], start=(k == 0), stop=(k == 7))
            P.act(tmp[f % 2], pb, AF.Relu)
            P.tt("pool" if f % 3 else "dve", h1T[:, f, :], tmp[f % 2], tmp[f % 2], ALU.mult)
        for t in range(4):
            for n in range(2):
                pb = pm2[(t % 2) * 2 + n]
                for f in range(32):
                    P.mm(pb, h1T[:, f, t * 128:(t + 1) * 128], w2[:, f, n * 512:(n + 1) * 512],
                         start=(f == 0), stop=(f == 31))
                P.tt("dve", ht[t][:, n * 512:(n + 1) * 512], pb, ht[t][:, n * 512:(n + 1) * 512], ALU.add)
            r0 = T * TT + t * 128
            if final:
                s = stt[t]
                P.act(junk, ht[t], AF.Square, accum=s[:, 4:5])
                _rms_scale(P, s[:, 4:5], s[:, 5:6], s[:, 6:7], s[:, 7:8], D, 1e-6)
                P.stt("dve", ht[t], ht[t], s[:, 7:8], fnw, ALU.mult, ALU.mult)
            o = P.dma("sp", dst[r0:r0 + 128, :], ht[t], "io%d" % t)
            if out_stores is not None:
                out_stores.append(o)


def _phase_hyb(P, A, PS, nc, layer, src, dst, S, w_in_d, w_out_d, nmix_d, hybp_d, rope_d, small_load, cst_load,
               identb, identf, out_stores):
    j = layer // 2
    NCH = S // L
    w_in = A.alloc([128, 8, HYB_PROJ], BF16, "w_in")
    w_out = A.alloc([128, 8, D], BF16, "w_out")
    nwc = small_load([128, 8], nmix_d[layer], "nwc")
    _load_weight(P, w_in, w_in_d[j], 8, 1, "w1")
    _load_weight(P, w_out, w_out_d[j], 8, 4, "w2")
    for k in range(8):
        _fold(P, k, w_in[:, k, :], nwc[:, k:k + 1])
    triNeg = cst_load("triNeg")
    maskA = cst_load("maskA")
    maskB = cst_load("maskB")
    e127 = cst_load("e127")
    decayT = cst_load("decayT")
    wq8 = cst_load("wq8")
    wk = cst_load("wk")
    cd = cst_load("cd")
    onesb = cst_load("ones", BF16)
    rope = small_load([128, 2 * NCH * 32], rope_d, "rope")
    cosT = rope[:, 0:NCH * 32].re("p (c f) -> p c f", f=32)
    sinT = rope[:, NCH * 32:2 * NCH * 32].re("p (c f) -> p c f", f=32)
    small = small_load([128, 8 + 1024], hybp_d[j], "hybsmall")
    bias8 = small[:, 0:8]
    mnw = small[:, 8:8 + 512]
    rnw = small[:, 8 + 512:8 + 1024]

    hc = [A.alloc([128, D], F32, "hc%d" % i) for i in range(2)]
    u = A.alloc([128, D], BF16, "u")
    uT = A.alloc([128, 8, 128], BF16, "uT")
    junk = A.alloc([128, D], BF16, "junk")
    proj_l = [A.alloc([128, HYB_PROJ], F32, "proj%d" % i) for i in range(2)]
    stt = A.alloc([128, 8], F32, "stt")
    def g(n, name):
        return A.alloc([128, n], F32, name)
    gif = g(8, "gif")
    ee = g(4, "ee")
    lsp = g(4, "lsp")
    av = g(4, "av")
    gm = g(8, "gm")
    cm = g(4, "cm")
    negM = g(4, "negM")
    bM = g(4, "bM")
    sel = g(8, "sel")
    mprev = g(4, "mprev")
    ea = g(8, "ea")
    eb = g(8, "eb")
    eo1 = g(8, "eo1")
    eo2 = g(8, "eo2")
    sci, emt = eo1[:, 0:4], eo1[:, 4:8]
    w2, sold = eo2[:, 0:4], eo2[:, 4:8]
    tmpA = A.alloc([128, 4, 128], F32, "tmpA")
    Dt = A.alloc([128, 4, 128], F32, "Dt")
    q8m = A.alloc([128, 256], BF16, "q8m")
    qsm = A.alloc([128, 256], BF16, "qsm")
    km = A.alloc([128, 256], BF16, "km")
    kwm = A.alloc([128, 256], BF16, "kwm")
    vm = A.alloc([128, 512], BF16, "vm")
    rq = A.alloc([128, 256], F32, "rq")
    rk = A.alloc([128, 256], F32, "rk")
    rt = [A.alloc([128, 128], F32, "rt%d" % i) for i in range(4)]
    q8r = A.alloc([128, 256], BF16, "q8r")
    qsr = A.alloc([128, 256], BF16, "qsr")
    kr = A.alloc([128, 256], BF16, "kr")
    kwr = A.alloc([128, 256], BF16, "kwr")
    vr = A.alloc([128, 512], BF16, "vr")
    kTm = A.alloc([128, 2, 128], BF16, "kTm")
    kTr = A.alloc([128, 2, 128], BF16, "kTr")
    qTm = A.alloc([128, 4, 128], BF16, "qTm")
    qTr = A.alloc([128, 4, 128], BF16, "qTr")
    qsTm = A.alloc([128, 4, 128], BF16, "qsTm")
    qsTr = A.alloc([128, 4, 128], BF16, "qsTr")
    St_m = A.alloc([128, 4, 128], BF16, "St_m")
    St_r = A.alloc([128, 4, 128], BF16, "St_r")
    hm = A.alloc([128, 4, 128], F32, "hm")
    hr = A.alloc([128, 4, 128], F32, "hr")
    sq = A.alloc([128, 4, 128], F32, "sq")
    d1 = g(4, "d1")
    d2 = g(4, "d2")
    rec = g(4, "rec")
    nst = A.alloc([128, 24], F32, "nst")
    nsr = A.alloc([128, 24], F32, "nsr")
    gw_m = A.alloc([128, 512], F32, "gw_m")
    gw_r = A.alloc([128, 512], F32, "gw_r")
    cat = A.alloc([128, D], BF16, "cat")
    catT = A.alloc([128, 8, 128], BF16, "catT")
    Cst = A.alloc([128, 2, 256], F32, "Cst")
    nstt = A.alloc([128, 2], F32, "nstt")
    Cbf = A.alloc([128, 2, 256], BF16, "Cbf")
    nbf = A.alloc([128, 2], BF16, "nbf")
    Rst = A.alloc([128, 2, 256], F32, "Rst")
    Rbf = A.alloc([128, 2, 256], BF16, "Rbf")

    pp = [PS.view(b, 0, 512) for b in (0, 1)]
    pT = PS.view(2, 0, 512, BF16)
    pTq = PS.view(6, 0, 256, BF16)
    pbc = PS.view(3, 0, 512, shape=[4, 128])
    pS_m = PS.view(4, 0, 512, shape=[4, 128])
    pS_r = PS.view(5, 0, 512, shape=[4, 128])
    pdC = PS.view(3, 0, 512, shape=[2, 256])
    pdR = PS.view(6, 0, 512, shape=[2, 256])
    p_b = PS.view(7, 0, 4, sub="b")
    p_sel = PS.view(7, 8, 8, sub="sel")
    p_den = PS.view(7, 16, 4, sub="den")
    p_dn = PS.view(7, 24, 2, sub="dn")

    for t_ in (Cst, nstt, Rst, mprev):
        P.memset("pool", t_, 0.0)
    for t_ in (Cbf, nbf, Rbf, qTm, qTr, qsTm, qsTr):
        P.memset("pool", t_, 0.0)

    blocks = [(b * 512, min(512, HYB_PROJ - b * 512)) for b in range(7)]
    def _front(c):
        h = hc[c % 2]
        r0 = c * L
        proj = proj_l[c % 2]
        P.dma("sp", h, src[r0:r0 + L, :], "io%d" % (c % 2))
        P.act(junk, h, AF.Square, accum=stt[:, 0:1])
        _rms_scale(P, stt[:, 0:1], stt[:, 1:2], stt[:, 2:3], stt[:, 3:4], D, 1e-6)
        P.ts("dve", u, h, stt[:, 3:4], ALU.mult)
        for k in range(8):
            P.tr(pT[:, k * 128:(k + 1) * 128], u[:, k * 128:(k + 1) * 128], identb)
        P.cp("act", uT, pT.re("p (a b) -> p a b", b=128))
        for bi, (c0, cn) in enumerate(blocks):
            pb = pp[bi % 2]
            for k in range(8):
                P.mm(pb[:, 0:cn], uT[:, k, :], w_in[:, k, c0:c0 + cn], start=(k == 0), stop=(k == 7))
            P.cp("dve" if bi % 2 else "act", proj[:, c0:c0 + cn], pb[:, 0:cn])
    def _core(c):
        h = hc[c % 2]
        r0 = c * L
        proj = proj_l[c % 2]
        P.tt("dve", gif, proj[:, 1024:1032], bias8, ALU.add)
        P.act(ee, gif[:, 4:8], AF.Exp, scale=-1.0)
        P.act(lsp, ee, AF.Ln, bias=1.0)
        P.mm(p_b, triNeg, lsp)
        P.cp("dve", gm[:, 0:4], p_b)
        P.tt("dve", av, gif[:, 0:4], p_b, ALU.subtract)
        for hh in range(4):
            P.mm(pbc[:, hh, :], av[:, hh:hh + 1].bc([128, 128]), identf)
        P.tt("dve", tmpA, pbc, maskA.re("p (o s) -> p o s", o=1).bc([128, 4, 128]), ALU.add)
        P.red("dve", cm, tmpA, ALU.max)
        P.tt("dve", gm[:, 4:8], cm, mprev, ALU.max)
        P.ts("dve", negM, gm[:, 4:8], -1.0, ALU.mult)
        for hh in range(4):
            P.mm(pbc[:, hh, :], negM[:, hh:hh + 1].bc([128, 128]), identf)
        P.mm(p_sel, e127, gm)
        for hh in range(4):
            P.stt("dve", Dt[:, hh, :], pbc[:, hh, :], av[:, hh:hh + 1], maskB, ALU.add, ALU.add)
        P.act(Dt, Dt, AF.Exp)
        P.tt("dve", ea[:, 0:4], mprev, gm[:, 4:8], ALU.subtract)
        P.tt("dve", ea[:, 4:8], negM, gm[:, 0:4], ALU.subtract)
        P.act(eo1, ea, AF.Exp)
        P.cp("dve", sel, p_sel)
        P.tt("dve", eb[:, 0:4], av, sel[:, 4:8], ALU.subtract)
        P.tt("dve", eb[:, 4:8], mprev, sel[:, 4:8], ALU.subtract)
        P.act(eo2, eb, AF.Exp)
        P.tt("dve", mprev, sel[:, 0:4], sel[:, 4:8], ALU.add)
        pq = proj[:, 0:256].re("p (h k) -> p h k", k=64)
        pk = proj[:, 256:512].re("p (h k) -> p h k", k=64)
        P.act(q8m, proj[:, 0:256], AF.Identity, scale=0.125)
        P.stt("dve", qsm.re("p (h k) -> p h k", k=64), pq, 0.125,
              sci.re("p (h o) -> p h o", o=1).bc([128, 4, 64]), ALU.mult, ALU.mult)
        P.cp("pool", km, proj[:, 256:512])
        P.cp("pool", vm, proj[:, 512:1024])
        cosc = cosT[:, c, :].re("p (o f) -> p o f", o=1).bc([128, 4, 32])
        sinc = sinT[:, c, :].re("p (o f) -> p o f", o=1).bc([128, 4, 32])
        for (srcc, dstt) in ((1544, rq), (1800, rk)):
            sv = proj[:, srcc:srcc + 256].re("p (h two f) -> p h two f", two=2, f=32)
            dv = dstt.re("p (h two f) -> p h two f", two=2, f=32)
            t1 = sv[:, :, 0, :]
            t2 = sv[:, :, 1, :]
            ta = rt[0].re("p (h f) -> p h f", f=32)
            tb = rt[1].re("p (h f) -> p h f", f=32)
            tc = rt[2].re("p (h f) -> p h f", f=32)
            td = rt[3].re("p (h f) -> p h f", f=32)
            P.tt("pool", ta, t1, cosc, ALU.mult)
            P.tt("pool", tb, t2, sinc, ALU.mult)
            P.tt("pool", dv[:, :, 0, :], ta, tb, ALU.subtract)
            P.tt("pool", tc, t1, sinc, ALU.mult)
            P.tt("pool", td, t2, cosc, ALU.mult)
            P.tt("pool", dv[:, :, 1, :], tc, td, ALU.add)
        P.act(q8r, rq, AF.Identity, scale=0.125)
        P.tt("pool", qsr.re("p (h k) -> p h k", k=64), rq.re("p (h k) -> p h k", k=64),
             wq8.re("p (h o) -> p h o", o=1).bc([128, 4, 64]), ALU.mult)
        P.cp("pool", kr, rk)
        P.tt("pool", kwr.re("p (h k) -> p h k", k=64), rk.re("p (h k) -> p h k", k=64),
             wk.re("p (h o) -> p h o", o=1).bc([128, 4, 64]), ALU.mult)
        P.cp("pool", vr, proj[:, 2056:2568])
        P.tt("pool", kwm.re("p (h k) -> p h k", k=64), pk, w2.re("p (h o) -> p h o", o=1).bc([128, 4, 64]), ALU.mult)
        tl = [q8m, km, q8r, kr, qsm, qsr]
        for i in range(4):
            for bb in range(2):
                P.tr(pT[:, (i * 2 + bb) * 128:(i * 2 + bb + 1) * 128], tl[i][:, bb * 128:(bb + 1) * 128], identb)
        for i in range(4, 6):
            for bb in range(2):
                P.tr(pTq[:, ((i - 4) * 2 + bb) * 128:((i - 4) * 2 + bb + 1) * 128], tl[i][:, bb * 128:(bb + 1) * 128], identb)
        pT3 = pT.re("p (a b) -> p a b", b=128)
        pTq3 = pTq.re("p (a b) -> p a b", b=128)

        def masked_evac(dstt, srcv, e0, e1):
            dv = dstt.re("p (b two) t -> p b two t", two=2)
            P.cp(e0, dv[0:64, :, 0, :], srcv[0:64])
            P.cp(e1, dv[64:128, :, 1, :], srcv[64:128])
        masked_evac(qTm, pT3[:, 0:2, :], "act", "dve")
        P.cp("act", kTm, pT3[:, 2:4, :])
        masked_evac(qTr, pT3[:, 4:6, :], "act", "dve")
        P.cp("dve", kTr, pT3[:, 6:8, :])
        masked_evac(qsTm, pTq3[:, 0:2, :], "act", "dve")
        masked_evac(qsTr, pTq3[:, 2:4, :], "act", "dve")

        def stv(t_, hh):
            return t_[(hh % 2) * 64:(hh % 2) * 64 + 64, hh // 2, (hh % 2) * 128:(hh % 2) * 128 + 128]

        def stc(t_, hh):
            return t_[:, hh // 2, (hh % 2) * 128:(hh % 2) * 128 + 128]

        def nv_(t_, hh):
            return t_[(hh % 2) * 64:(hh % 2) * 64 + 64, hh // 2:hh // 2 + 1]
        for hh in range(4):
            P.mm(pS_m[:, hh, :], kTm[:, hh // 2, :], qTm[:, hh, :])
        for hh in range(4):
            P.mm(pS_r[:, hh, :], kTr[:, hh // 2, :], qTr[:, hh, :])
        P.tt("dve", St_m, pS_m, Dt, ALU.mult)
        P.tt("dve", St_r, pS_r, decayT.re("p (h t) -> p h t", t=128), ALU.mult)
        for hh in range(4):
            P.mm(pS_m[:, hh, :], St_m[:, hh, :], vm[:, hh * 128:(hh + 1) * 128], start=True, stop=False)
            P.mm(pS_m[:, hh, :], qsTm[:, hh, :], stc(Cbf, hh), start=False, stop=True)
        for hh in range(4):
            P.mm(p_den[:, hh:hh + 1], St_m[:, hh, :], onesb[:, 0:1], start=True, stop=False)
            P.mm(p_den[:, hh:hh + 1], qsTm[:, hh, :], nbf[:, hh // 2:hh // 2 + 1], start=False, stop=True)
        for hh in range(4):
            P.mm(pS_r[:, hh, :], St_r[:, hh, :], vr[:, hh * 128:(hh + 1) * 128], start=True, stop=False)
            P.mm(pS_r[:, hh, :], qsTr[:, hh, :], stc(Rbf, hh), start=False, stop=True)
        P.ts("dve", d1, p_den, -1.0, ALU.mult)
        P.tt("dve", d1, d1, p_den, ALU.max)
        P.tt("dve", d2, d1, emt, ALU.max)
        P.recip(rec, d2)
        P.tt("dve", hm, pS_m, rec.re("p (h o) -> p h o", o=1).bc([128, 4, 128]), ALU.mult)
        P.act(gw_m, proj[:, 1032:1544], AF.Sigmoid)
        P.tt("pool", gw_m, gw_m, mnw, ALU.mult)
        _headnorm(P, hm, sq, nst, gw_m, cat[:, 0:512])
        P.cp("act", hr, pS_r)
        P.act(gw_r, proj[:, 2568:3080], AF.Silu)
        P.tt("pool", gw_r, gw_r, rnw, ALU.mult)
        _headnorm(P, hr, sq, nsr, gw_r, cat[:, 512:1024])
        for bb in range(2):
            P.mm(pdC[:, bb, :], kwm[:, bb * 128:(bb + 1) * 128], vm[:, bb * 256:(bb + 1) * 256])
        for bb in range(2):
            P.mm(p_dn[:, bb:bb + 1], kwm[:, bb * 128:(bb + 1) * 128], onesb[:, 0:1])
        for hh in range(4):
            so = sold[(hh % 2) * 64:(hh % 2) * 64 + 64, hh:hh + 1]
            P.stt("dve", stv(Cst, hh), stv(Cst, hh), so, stv(pdC, hh), ALU.mult, ALU.add)
            P.stt("dve", nv_(nstt, hh), nv_(nstt, hh), so, nv_(p_dn, hh), ALU.mult, ALU.add)
        P.cp("pool", Cbf, Cst)
        P.cp("pool", nbf, nstt)
        for bb in range(2):
            P.mm(pdR[:, bb, :], kwr[:, bb * 128:(bb + 1) * 128], vr[:, bb * 256:(bb + 1) * 256])
        for hh in range(4):
            cdv = cd[(hh % 2) * 64:(hh % 2) * 64 + 64, hh:hh + 1]
            P.stt("dve", stv(Rst, hh), stv(Rst, hh), cdv, stv(pdR, hh), ALU.mult, ALU.add)
        P.cp("pool", Rbf, Rst)
        for k in range(8):
            P.tr(pT[:, k * 128:(k + 1) * 128], cat[:, k * 128:(k + 1) * 128], identb)
        P.cp("act", catT, pT.re("p (a b) -> p a b", b=128))
        for n in range(2):
            for k in range(8):
                P.mm(pp[n], catT[:, k, :], w_out[:, k, n * 512:(n + 1) * 512], start=(k == 0), stop=(k == 7))
            P.tt("dve", h[:, n * 512:(n + 1) * 512], pp[n], h[:, n * 512:(n + 1) * 512], ALU.add)
        o = P.dma("sp", dst[r0:r0 + L, :], h, "io%d" % (c % 2))
        if out_stores is not None:
            out_stores.append(o)

    _front(0)
    for c in range(NCH):
        if c + 1 < NCH:
            _front(c + 1)
        _core(c)


def _headnorm(P, hv, sq, ns, gw, outv):
    s1 = ns[:, 0:4]
    s2 = ns[:, 4:8]
    mean = ns[:, 8:12]
    var = ns[:, 12:16]
    m2 = ns[:, 16:20]
    rstd = ns[:, 20:24]
    P.red("dve", s1, hv, ALU.add)
    P.tt("dve", sq, hv, hv, ALU.mult)
    P.red("dve", s2, sq, ALU.add)
    P.ts("dve", mean, s1, 1.0 / 128, ALU.mult)
    P.tt("dve", m2, mean, mean, ALU.mult)
    P.stt("dve", var, s2, 1.0 / 128, m2, ALU.mult, ALU.subtract)
    P.ts("dve", var, var, 1e-5, ALU.add)
    P.act(m2, var, AF.Ln)
    P.act(rstd, m2, AF.Exp, scale=-0.5)
    P.tt("dve", hv, hv, mean.re("p (h o) -> p h o", o=1).bc([128, 4, 128]), ALU.subtract)
    P.tt("dve", hv, hv, rstd.re("p (h o) -> p h o", o=1).bc([128, 4, 128]), ALU.mult)
    P.tt("dve", outv.re("p (h k) -> p h k", k=128), hv, gw.re("p (h k) -> p h k", k=128), ALU.mult)


def _phase_ssd(P, A, PS, nc, layer, src, dst, S, w_in_d, w_out_d, nmix_d, ssdp_d, small_load, cst_load,
               identb, identf, out_stores):
    j = layer // 2
    NCH = S // L
    w_in = A.alloc([128, 8, SSD_PROJ], BF16, "w_in")
    w_out = A.alloc([128, 16, D], BF16, "w_out")
    nwc = small_load([128, 8], nmix_d[layer], "nwc")
    small = small_load([128, 272], ssdp_d[j], "ssdsmall")
    cw = small[:, 0:128].re("p (i t) -> p i t", t=4)
    cb = small[:, 128:160]
    dtb = small[:, 160:192]
    alog = small[:, 192:224]
    dsk = small[:, 224:256]
    nsw = small[:, 256:272]
    _load_weight(P, w_in, w_in_d[j], 8, 1, "w1")
    _load_weight(P, w_out, w_out_d[j], 16, 4, "w2")
    for k in range(8):
        _fold(P, k, w_in[:, k, :], nwc[:, k:k + 1])
    for k in range(16):
        _fold(P, k, w_out[:, k, :], nsw[:, k:k + 1])
    triPos = cst_load("triPos")
    triPosb = A.alloc([128, 128], BF16, "triPosb")
    P.cp("pool", triPosb, triPos)
    mask01B = cst_load("mask01B")
    e127 = cst_load("e127")
    Aneg = A.alloc([128, 32], F32, "Aneg")
    P.act(Aneg, alog, AF.Exp)
    P.ts("dve", Aneg, Aneg, -1.0, ALU.mult)

    hc = [A.alloc([128, D], F32, "hc%d" % i) for i in range(2)]
    u = A.alloc([128, D], BF16, "u")
    uT = A.alloc([128, 8, 128], BF16, "uT")
    stt = A.alloc([128, 8], F32, "stt")
    xpre = [A.alloc([128, 4, 131], BF16, "xpre%d" % q) for q in range(8)]
    xpre_all = V(None, [x_.bufs[0] for x_ in xpre])
    accw, acc = [], []
    for s_ in range(2):
        t_ = A.alloc([128, 4, 128], F32, "acc%d" % s_)
        bl = [Buf("acc%d_%d" % (s_, ii)) for ii in range(4)]
        accw.append(V(t_.ap, bl))
        acc.append([V(t_.ap[:, ii, :], [bl[ii]]) for ii in range(4)])
    ptmp = [A.alloc([128, 128], F32, "ptmp%d" % i) for i in range(4)]
    xTt = A.alloc([128, 4, 128], BF16, "xTt")
    BT = A.alloc([128, 8, 128], BF16, "BT")
    CT = A.alloc([128, 8, 128], BF16, "CT")
    x_tok = A.alloc([128, 2048], BF16, "x_tok")
    xdt = A.alloc([128, 2048], BF16, "xdt")
    B_tok = A.alloc([128, 1024], BF16, "B_tok")
    zs = A.alloc([128, 2048], BF16, "zs")
    Ee = A.alloc([128, 4, 128], F32, "Ee")
    E2 = A.alloc([128, 4, 128], F32, "E2")
    CBm = A.alloc([128, 128], F32, "CBm")
    MT = A.alloc([128, 4, 128], BF16, "MT")
    CTs = A.alloc([128, 4, 128], BF16, "CTs")
    xw = A.alloc([128, 256], BF16, "xw")
    ytmp = A.alloc([128, 256], F32, "ytmp")
    Ee_l = [Ee, accw[0]]
    E2_l = [E2, accw[1]]
    MT_l = [MT, xTt]
    CTs_l = [CTs, A.alloc([128, 4, 128], BF16, "CTs2")]
    CBm_l = [CBm, A.alloc([128, 128], F32, "CBm2")]
    ytmp_l = [ytmp, A.alloc([128, 256], F32, "ytmp2")]
    xw_l = [xw, A.alloc([128, 256], BF16, "xw2")]
    ygpre = A.alloc([128, 2048], BF16, "ygpre")
    ygT = V(xdt.ap.rearrange("p (a b) -> p a b", b=128), xdt.bufs)
    junk = V(xdt.ap[:, 0:1024], xdt.bufs)
    Hs = [A.alloc([128, 256], F32, "H%d" % g) for g in range(8)]
    Hb = [A.alloc([128, 256], BF16, "Hb%d" % g) for g in range(8)]

    def sm(n, name):
        return A.alloc([128, n], F32, name)
    dtp = sm(32, "dtp")
    edt = sm(32, "edt")
    dt = sm(32, "dt")
    dtA = sm(32, "dtA")
    dhi = A.alloc([128, 32], BF16, "dhi")
    dlo = A.alloc([128, 32], BF16, "dlo")
    cs = sm(32, "cs")
    tea = sm(32, "tea")
    te = sm(32, "te")
    dec = sm(32, "dec")
    ssq8 = sm(8, "ssq8")
    ms8 = sm(8, "ms8")
    ln8 = sm(8, "ln8")
    rstd8 = sm(8, "rstd8")

    pp = [PS.view(b_, 0, 512) for b_ in (0, 1)]
    pf = [PS.view(b_, 0, 512, shape=[4, 128]) for b_ in (2, 3)]
    pT = PS.view(4, 0, 512, BF16)
    pcsb_l = [PS.view(5, 0, 512, shape=[4, 128]), PS.view(2, 0, 512, shape=[4, 128])]
    pCB = PS.view(6, 0, 128)
    p_cs = PS.view(6, 128, 32)
    p_csl = PS.view(6, 160, 32)
    py_l = [PS.view(7, 0, 256), PS.view(3, 0, 256)]
    pdH_l = [PS.view(0, 0, 256), PS.view(1, 0, 256)]

    for g in range(8):
        P.memset("pool", Hs[g], 0.0)
        P.memset("pool", Hb[g], 0.0)
    for q in range(8):
        P.memset("pool", xpre[q], 0.0)

    def _head(c):
        h = hc[c % 2]
        r0 = c * L
        P.dma("sp", h, src[r0:r0 + L, :], "io%d" % (c % 2))
        P.act(junk, h, AF.Square, accum=stt[:, 0:1])
        _rms_scale(P, stt[:, 0:1], stt[:, 1:2], stt[:, 2:3], stt[:, 3:4], D, 1e-6)
        P.ts("dve", u, h, stt[:, 3:4], ALU.mult)
        for k in range(8):
            P.tr(pT[:, k * 128:(k + 1) * 128], u[:, k * 128:(k + 1) * 128], identb)
        P.cp("act", uT, pT.re("p (a b) -> p a b", b=128))
        for bi in range(4):
            pb = pp[bi % 2]
            for k in range(8):
                P.mm(pb, uT[:, k, :], w_in[:, k, bi * 512:(bi + 1) * 512], start=(k == 0), stop=(k == 7))
            P.act(zs[:, bi * 512:(bi + 1) * 512], pb, AF.Silu)
        pb = pp[0]
        for k in range(8):
            P.mm(pb[:, 0:32], uT[:, k, :], w_in[:, k, 6144:6176], start=(k == 0), stop=(k == 7))
        P.tt("dve", dtp, pb[:, 0:32], dtb, ALU.add)
        P.act(edt, dtp, AF.Exp)
        P.act(dt, edt, AF.Ln, bias=1.0)
        P.tt("dve", dtA, dt, Aneg, ALU.mult)
        P.mm(p_cs, triPos, dtA)
        P.cp("dve", cs, p_cs)
        P.mm(p_csl, e127, cs)
        P.tt("dve", tea, p_csl, cs, ALU.subtract)
        P.act(te, tea, AF.Exp)
        P.act(dec, p_csl, AF.Exp)
        if c > 0:
            for q in range(8):
                P.cp("pool", xpre[q][:, :, 0:3], xpre[q][:, :, 128:131])
        for q in range(8):
            pfb = pf[q % 2]
            for ii in range(4):
                i = q * 4 + ii
                for k in range(8):
                    P.mm(pfb[:, ii, :], w_in[:, k, 2048 + i * 128:2048 + (i + 1) * 128], uT[:, k, :],
                         start=(k == 0), stop=(k == 7))
            P.cp("act" if q % 2 else "dve", xpre[q][:, :, 3:131], pfb)
    def _mid(c):
        h = hc[c % 2]
        r0 = c * L
        for q in range(8):
            ac = acc[q % 2]
            xis = [xpre[q][:, ii, :] for ii in range(4)]
            for ii in range(4):
                i = q * 4 + ii
                P.act(ac[ii], xis[ii][:, 3:131], AF.Identity, bias=cb[:, i:i + 1], scale=cw[:, i, 3:4])
            for ii in range(4):
                i = q * 4 + ii
                P.act(ptmp[ii], xis[ii][:, 0:128], AF.Identity, scale=cw[:, i, 0:1])
            for ii in range(4):
                i = q * 4 + ii
                P.stt("dve", ac[ii], xis[ii][:, 2:130], cw[:, i, 2:3], ac[ii], ALU.mult, ALU.add)
            for ii in range(4):
                i = q * 4 + ii
                P.stt("dve", ac[ii], xis[ii][:, 1:129], cw[:, i, 1:2], ac[ii], ALU.mult, ALU.add)
            for ii in range(4):
                P.tt("pool", ac[ii], ac[ii], ptmp[ii], ALU.add)
            if q < 4:
                dsts = [xTt[:, ii, :] for ii in range(4)]
            elif q < 6:
                dsts = [BT[:, (q - 4) * 4 + ii, :] for ii in range(4)]
            else:
                dsts = [CT[:, (q - 6) * 4 + ii, :] for ii in range(4)]
            if q < 4:
                dwhole = xTt
            elif q < 6:
                dwhole = BT[:, (q - 4) * 4:(q - 4) * 4 + 4, :]
            else:
                dwhole = CT[:, (q - 6) * 4:(q - 6) * 4 + 4, :]
            P.act(dwhole, accw[q % 2], AF.Silu)
            if q < 6:
                for ii in range(4):
                    P.tr(pT[:, ii * 128:(ii + 1) * 128], dsts[ii], identb)
                if q < 4:
                    P.cp("dve", x_tok[:, q * 512:(q + 1) * 512], pT[:, 0:512])
                else:
                    P.cp("dve", B_tok[:, (q - 4) * 512:(q - 3) * 512], pT[:, 0:512])
        x3 = x_tok.re("p (h k) -> p h k", k=64)
        P.tt("dve", xdt.re("p (h k) -> p h k", k=64), x3, dt.re("p (h o) -> p h o", o=1).bc([128, 32, 64]), ALU.mult)
        P.tt("pool", x3, x3, dsk.re("p (h o) -> p h o", o=1).bc([128, 32, 64]), ALU.mult)
        def _grp_a(g):
            pcsb = pcsb_l[g % 2]
            py = py_l[g % 2]
            pdH = pdH_l[g % 2]
            Ee = Ee_l[g % 2]
            E2 = E2_l[g % 2]
            MT = MT_l[g % 2]
            CTs = CTs_l[g % 2]
            CBm = CBm_l[g % 2]
            ytmp = ytmp_l[g % 2]
            xw = xw_l[g % 2]
            for jj in range(4):
                hd = 4 * g + jj
                P.mm(pcsb[:, jj, :], dtA[:, hd:hd + 1].bc([128, 128]), triPos)
            P.mm(pCB, BT[:, g, :], CT[:, g, :])
            if SSD_SUB <= 'a':
                return
            P.tt("dve", CBm, pCB, mask01B, ALU.mult)
            for jj in range(4):
                P.ts("dve", Ee[:, jj, :], pcsb[:, jj, :], cs[:, 4 * g + jj:4 * g + jj + 1], ALU.subtract, 0.0, ALU.min)
            P.act(E2, pcsb, AF.Exp)
            P.act(Ee, Ee, AF.Exp)
        def _grp_b(g):
            pcsb = pcsb_l[g % 2]
            py = py_l[g % 2]
            pdH = pdH_l[g % 2]
            Ee = Ee_l[g % 2]
            E2 = E2_l[g % 2]
            MT = MT_l[g % 2]
            CTs = CTs_l[g % 2]
            CBm = CBm_l[g % 2]
            ytmp = ytmp_l[g % 2]
            xw = xw_l[g % 2]
            P.tt("dve", MT, Ee, CBm.re("p (o t) -> p o t", o=1).bc([128, 4, 128]), ALU.mult)
            P.tt("dve", CTs, E2, CT[:, g, :].re("p (o t) -> p o t", o=1).bc([128, 4, 128]), ALU.mult)
            for jj in range(4):
                hd = 4 * g + jj
                P.mm(py[:, jj * 64:(jj + 1) * 64], MT[:, jj, :], xdt[:, hd * 64:(hd + 1) * 64], start=True, stop=False)
                P.mm(py[:, jj * 64:(jj + 1) * 64], CTs[:, jj, :], Hb[g][:, jj * 64:(jj + 1) * 64], start=False, stop=True)
            if SSD_SUB <= 'c':
                return
            gs = slice(g * 256, (g + 1) * 256)
            P.tt("dve", ytmp, py, x_tok[:, gs], ALU.add)
            P.tt("pool", ygpre[:, gs], zs[:, gs], ytmp, ALU.mult)
            P.act(u[:, 0:256], ygpre[:, gs], AF.Square, accum=ssq8[:, g:g + 1])
            if SSD_SUB <= 'd':
                return
            P.tt("pool", xw.re("p (r k) -> p r k", k=64), xdt[:, gs].re("p (r k) -> p r k", k=64),
                 te[:, 4 * g:4 * g + 4].re("p (r o) -> p r o", o=1).bc([128, 4, 64]), ALU.mult)
            P.mm(pdH, B_tok[:, g * 128:(g + 1) * 128], xw)
            P.tt("pool", Hs[g].re("p (r k) -> p r k", k=64), Hs[g].re("p (r k) -> p r k", k=64),
                 dec[:, 4 * g:4 * g + 4].re("p (r o) -> p r o", o=1).bc([128, 4, 64]), ALU.mult)
            P.tt("dve", Hs[g], Hs[g], pdH, ALU.add)
            P.cp("pool", Hb[g], Hs[g])
        _grp_a(0)
        for g in range(8):
            if g + 1 < 8:
                _grp_a(g + 1)
            _grp_b(g)
    def _tail(c):
        h = hc[c % 2]
        r0 = c * L
        _rms_scale(P, ssq8, ms8, ln8, rstd8, 256, 1e-5)
        yg3 = ygpre.re("p (g k) -> p g k", k=256)
        P.tt("dve", yg3, yg3, rstd8.re("p (g o) -> p g o", o=1).bc([128, 8, 256]), ALU.mult)
        for rnd in range(2):
            for k in range(8):
                kk = rnd * 8 + k
                P.tr(pT[:, k * 128:(k + 1) * 128], ygpre[:, kk * 128:(kk + 1) * 128], identb)
            P.cp("act", ygT[:, rnd * 8:(rnd + 1) * 8, :], pT.re("p (a b) -> p a b", b=128))
        for n in range(2):
            for k in range(16):
                P.mm(pp[n], ygT[:, k, :], w_out[:, k, n * 512:(n + 1) * 512], start=(k == 0), stop=(k == 15))
            P.tt("dve", h[:, n * 512:(n + 1) * 512], pp[n], h[:, n * 512:(n + 1) * 512], ALU.add)
        o = P.dma("sp", dst[r0:r0 + L, :], h, "io%d" % (c % 2))
        if out_stores is not None:
            out_stores.append(o)

    _head(0)
    for c in range(NCH):
        _mid(c)
        if c + 1 < NCH:
            _head(c + 1)
        _tail(c)


FULL_PLAN = [("hyb", 0), ("mlp", 0), ("ssd", 1), ("mlp", 1), ("hyb", 2), ("mlp", 2), ("ssd", 3), ("mlp", 3)]


def _col(v, k):
    return np.ascontiguousarray(np.asarray(v, np.float32).reshape(k, 128).T)


def _bc(v):
    v = np.asarray(v, np.float32).reshape(1, -1)
    return np.ascontiguousarray(np.broadcast_to(v, (128, v.shape[1])))


def make_in_maps(inputs, S, n_cores):
    cst_np, _ = _build_consts()
    f = lambda k: np.ascontiguousarray(np.asarray(inputs[k], np.float32))
    nmix = np.stack([_col(inputs["norm_mix_w"][l], 8) for l in range(4)])
    nmlp = np.stack([_col(inputs["norm_mlp_w"][l], 8) for l in range(4)])
    hyb_small = np.stack([np.concatenate([_bc(inputs["mlstm_i_bias"][j]), _bc(inputs["mlstm_f_bias"][j]),
                                          _bc(inputs["mlstm_norm_w"][j]), _bc(inputs["ret_norm_w"][j])], axis=1)
                          for j in range(2)])
    ssd_small = []
    for j in range(2):
        cw = np.asarray(inputs["ssd_conv_w"][j], np.float32)
        cwl = cw.reshape(4, 32, 128).transpose(2, 1, 0).reshape(128, 128)
        cb = _col(inputs["ssd_conv_b"][j], 32)
        ssd_small.append(np.concatenate([cwl, cb, _bc(inputs["ssd_dt_bias"][j]), _bc(inputs["ssd_a_log"][j]),
                                         _bc(inputs["ssd_d"][j]), _col(inputs["ssd_norm_w"][j], 16)], axis=1))
    ssd_small = np.ascontiguousarray(np.stack(ssd_small).astype(np.float32))
    shared = {
        "hyb_w_in": f("hyb_w_in"), "hyb_w_out": f("hyb_w_out"), "ssd_w_in": f("ssd_w_in"), "ssd_w_out": f("ssd_w_out"),
        "mlp_w1": f("mlp_w1"), "mlp_w2": f("mlp_w2"), "cst": cst_np, "rope": _rope_tables(S),
        "nmix_col": nmix, "nmlp_col": nmlp, "fnw_b": _bc(inputs["final_norm_w"]),
        "hyb_small": np.ascontiguousarray(hyb_small.astype(np.float32)), "ssd_small": ssd_small,
    }
    x = np.asarray(inputs["x"], np.float32)
    maps = []
    for i in range(n_cores):
        m = dict(shared)
        m["x"] = np.ascontiguousarray(x[i, :S])
        maps.append(m)
    return maps


_NC_CACHE = {}


def kernel(**inputs):
    S = inputs["x"].shape[1]
    B = inputs["x"].shape[0]
    key = (S, "full")
    if key not in _NC_CACHE:
        _NC_CACHE[key] = build_program(S, FULL_PLAN, final_norm=True)
    nc = _NC_CACHE[key]
    maps = make_in_maps(inputs, S, B)
    res = run_bass_kernel_spmd(nc, maps, core_ids=list(range(B)))
    return np.stack([np.asarray(r["out"], np.float32) for r in res.results], axis=0)
```
